# Optimizing a Trainium2 kernel written in Bass

```python
import jax, jax.numpy as jnp
from jax import lax
import numpy as np

D_MODEL = 2048
BATCH = 2
SEQ = 8192
DEPTH = 4

CHUNK = 64
Q_BLOCK = 128
SB_HEADS = 8
SB_HEAD_DIM = 128
SB_WIDTH = SB_HEADS * SB_HEAD_DIM
SGU_GROUPS = 8
SGU_GROUP_DIM = 128
SGU_WIDTH = SGU_GROUPS * SGU_GROUP_DIM
SGU_LEN = 128
D_FF = 4 * D_MODEL
IN_COLS = 3 * SB_WIDTH + 2 * SGU_WIDTH + 2 * D_MODEL
EPS = 1e-6

kernel_name = "hybrid_stickbreak_sgu_block"


def rms_norm(x, g):
    xf = x.astype(jnp.float32)
    y = xf * lax.rsqrt(jnp.mean(xf * xf, axis=-1, keepdims=True) + EPS)
    return (y * g.astype(jnp.float32)).astype(x.dtype)


def layer_norm(x, g, b):
    xf = x.astype(jnp.float32)
    mu = jnp.mean(xf, axis=-1, keepdims=True)
    xc = xf - mu
    y = xc * lax.rsqrt(jnp.mean(xc * xc, axis=-1, keepdims=True) + EPS)
    return (y * g.astype(jnp.float32) + b.astype(jnp.float32)).astype(x.dtype)


def stick_breaking_attention(q, k, v):
    seq = q.shape[2]
    scale = SB_HEAD_DIM ** -0.5
    outs = []
    for blk in range(seq // Q_BLOCK):
        q0 = blk * Q_BLOCK
        kend = q0 + Q_BLOCK
        qb = q[:, :, q0:kend].astype(jnp.float32)
        kb = k[:, :, :kend].astype(jnp.float32)
        vb = v[:, :, :kend]
        z = jnp.einsum('bhtd,bhsd->bhts', qb, kb) * scale
        t_idx = q0 + jnp.arange(Q_BLOCK)[:, None]
        s_idx = jnp.arange(kend)[None, :]
        past = s_idx < t_idx
        log_keep = jnp.where(past, jax.nn.log_sigmoid(-z), 0.0)
        tail = lax.cumsum(log_keep, axis=3, reverse=True) - log_keep
        log_a = jax.nn.log_sigmoid(z) + tail
        a = jnp.where(past, jnp.exp(log_a), 0.0)
        outs.append(jnp.einsum('bhts,bhsd->bhtd', a.astype(v.dtype), vb))
    return jnp.concatenate(outs, axis=2)


def spatial_gating(u, v, ln_g, ln_b, w_s, b_s):
    bsz, seq, _ = v.shape
    v = layer_norm(v, ln_g, ln_b)
    vc = v.reshape(bsz, seq // SGU_LEN, SGU_LEN, SGU_GROUPS, SGU_GROUP_DIM)
    pos = jnp.arange(SGU_LEN)
    mask = (pos[None, :] // CHUNK) <= (pos[:, None] // CHUNK)
    w = jnp.where(mask[None], w_s, jnp.zeros_like(w_s))
    mixed = jnp.einsum('gij,bcjgd->bcigd', w, vc) + b_s.T[:, :, None]
    return u * mixed.reshape(bsz, seq, SGU_WIDTH)


def setup_inputs(seed: int = 0) -> dict:
    key = jax.random.key(seed)
    ks = jax.random.split(key, 16)
    f32 = jnp.float32
    nrm = lambda k, shape, s: jax.random.normal(k, shape, f32) * s
    return {
        "x": nrm(ks[0], (BATCH, SEQ, D_MODEL), 1.0),
        "g_mix": 1.0 + nrm(ks[1], (DEPTH, D_MODEL), 0.05),
        "w_in": nrm(ks[2], (DEPTH, D_MODEL, IN_COLS), D_MODEL ** -0.5),
        "g_q": 1.0 + nrm(ks[3], (DEPTH, SB_HEADS, SB_HEAD_DIM), 0.05),
        "g_k": 1.0 + nrm(ks[4], (DEPTH, SB_HEADS, SB_HEAD_DIM), 0.05),
        "sgu_ln_g": 1.0 + nrm(ks[5], (DEPTH, SGU_WIDTH), 0.05),
        "sgu_ln_b": nrm(ks[6], (DEPTH, SGU_WIDTH), 0.02),
        "w_spatial": nrm(ks[7], (DEPTH, SGU_GROUPS, SGU_LEN, SGU_LEN), SGU_LEN ** -0.5),
        "b_spatial": 1.0 + nrm(ks[8], (DEPTH, SGU_GROUPS, SGU_LEN), 0.05),
        "w_oa": nrm(ks[9], (DEPTH, SB_WIDTH, D_MODEL), SB_WIDTH ** -0.5),
        "w_ob": nrm(ks[10], (DEPTH, SGU_WIDTH, D_MODEL), SGU_WIDTH ** -0.5),
        "w_out": nrm(ks[11], (DEPTH, D_MODEL, D_MODEL), D_MODEL ** -0.5),
        "g_ff": 1.0 + nrm(ks[12], (DEPTH, D_MODEL), 0.05),
        "w_ff1": nrm(ks[13], (DEPTH, D_MODEL, D_FF), D_MODEL ** -0.5),
        "w_ff2": nrm(ks[14], (DEPTH, D_FF, D_MODEL), D_FF ** -0.5),
    }


def reference(x, g_mix, w_in, g_q, g_k, sgu_ln_g, sgu_ln_b, w_spatial, b_spatial,
              w_oa, w_ob, w_out, g_ff, w_ff1, w_ff2):
    bsz, seq, _ = x.shape
    splits = [SB_WIDTH, 2 * SB_WIDTH, 3 * SB_WIDTH,
              3 * SB_WIDTH + SGU_WIDTH, 3 * SB_WIDTH + 2 * SGU_WIDTH,
              3 * SB_WIDTH + 2 * SGU_WIDTH + D_MODEL]
    for l in range(DEPTH):
        h = rms_norm(x, g_mix[l])
        proj = h @ w_in[l]
        q, k, v_sb, u, v_sg, gate_a, gate_b = jnp.split(proj, splits, axis=-1)

        q = rms_norm(q.reshape(bsz, seq, SB_HEADS, SB_HEAD_DIM), g_q[l])
        k = rms_norm(k.reshape(bsz, seq, SB_HEADS, SB_HEAD_DIM), g_k[l])
        v_sb = v_sb.reshape(bsz, seq, SB_HEADS, SB_HEAD_DIM)
        o = stick_breaking_attention(q.transpose(0, 2, 1, 3), k.transpose(0, 2, 1, 3),
                                     v_sb.transpose(0, 2, 1, 3))
        y_a = o.transpose(0, 2, 1, 3).reshape(bsz, seq, SB_WIDTH) @ w_oa[l]

        u = jax.nn.gelu(u, approximate=False)
        v_sg = jax.nn.gelu(v_sg, approximate=False)
        s = spatial_gating(u, v_sg, sgu_ln_g[l], sgu_ln_b[l], w_spatial[l], b_spatial[l])
        y_b = s @ w_ob[l]

        merged = jax.nn.sigmoid(gate_a) * y_a + jax.nn.sigmoid(gate_b) * y_b
        x = x + merged @ w_out[l]

        h2 = rms_norm(x, g_ff[l])
        x = x + jnp.square(jax.nn.relu(h2 @ w_ff1[l])) @ w_ff2[l]
    return x
```

```python
import numpy as np
from contextlib import ExitStack, contextmanager
import ml_dtypes

import concourse.bass as bass
import concourse.mybir as mybir
from concourse.bass_utils import run_bass_kernel_spmd

F32 = mybir.dt.float32
BF16 = mybir.dt.bfloat16
AF = mybir.ActivationFunctionType
ALU = mybir.AluOpType

D = 2048
NH = 8
DH = 128
SBW = 1024
SGW = 1024
DFF = 8192
INC = 9216
EPS = 1e-6
TT = 1024
NCORES = 8
NW = 3


class Sem:
    def __init__(self, nc, name):
        self.h = nc.alloc_semaphore(name=name)
        self.count = 0


class Eng:
    def __init__(self, nc, name, e):
        self.e = e
        self.name = name
        self.sem = Sem(nc, "prog_" + name)
        self.seen = {}

    def wait(self, sem, val):
        if val <= 0 or self.seen.get(sem, 0) >= val:
            return
        self.e.wait_ge(sem.h, val)
        self.seen[sem] = val


class Buf:
    def __init__(self, t, fence):
        self.t = t
        self.w = None
        self.r = dict(fence)


class Scope:
    def __init__(self, kb):
        self.kb = kb
        self.es = ExitStack()
        self.bufs = []

    def sb(self, name, shape, dtype):
        kb = self.kb
        kb.uid += 1
        t = self.es.enter_context(kb.nc.sbuf_tensor(f"{name}_{kb.uid}", list(shape), dtype))
        b = Buf(t, kb.fence)
        self.bufs.append(b)
        return b

    def close(self):
        kb = self.kb
        for b in self.bufs:
            if b.w is not None:
                kb.fence[b.w[0]] = max(kb.fence.get(b.w[0], 0), b.w[1])
            for s, v in b.r.items():
                kb.fence[s] = max(kb.fence.get(s, 0), v)
        self.es.close()


class KB:
    def __init__(self, nc):
        self.nc = nc
        self.uid = 0
        self.fence = {}
        self.PE = Eng(nc, "pe", nc.tensor)
        self.ACT = Eng(nc, "act", nc.scalar)
        self.DVE = Eng(nc, "dve", nc.vector)
        self.POOL = Eng(nc, "pool", nc.gpsimd)
        self.SP = Eng(nc, "sp", nc.sync)
        self.engs = [self.PE, self.ACT, self.DVE, self.POOL, self.SP]
        self.dpool = {}
        self.dpi = {}
        for q in (self.SP, self.POOL, self.ACT):
            self.dpool[q] = [Sem(nc, f"dma_{q.name}_{i}") for i in range(8)]
            self.dpi[q] = 0
        self.gbufs = []

    @contextmanager
    def scope(self):
        s = Scope(self)
        try:
            yield s
        finally:
            s.close()

    def _deps(self, eng, reads, writes):
        for b in reads:
            if b.w is not None:
                self._w(eng, b.w)
        for b in writes:
            if b.w is not None:
                self._w(eng, b.w)
            for s, v in b.r.items():
                self._w(eng, (s, v))

    def _w(self, eng, st):
        sem, val = st
        if sem is eng.sem and eng is self.PE:
            return
        eng.wait(sem, val)

    def _stamp(self, st, reads, writes):
        for b in reads:
            b.r[st[0]] = max(b.r.get(st[0], 0), st[1])
        for b in writes:
            b.w = st
            b.r = {}

    def op(self, eng, fn, reads=(), writes=(), inc=True):
        self._deps(eng, reads, writes)
        ins = fn()
        if inc:
            eng.sem.count += 1
            ins.then_inc(eng.sem.h, 1)
            st = (eng.sem, eng.sem.count)
        else:
            st = (eng.sem, eng.sem.count + 1)
        self._stamp(st, reads, writes)

    def dma(self, q, out_ap, in_ap, reads=(), writes=()):
        self._deps(q, reads, writes)
        pool = self.dpool[q]
        sem = pool[self.dpi[q] % len(pool)]
        self.dpi[q] += 1
        q.wait(sem, sem.count)
        ins = q.e.dma_start(out=out_ap, in_=in_ap)
        sem.count += 16
        ins.then_inc(sem.h, 16)
        self._stamp((sem, sem.count), reads, writes)

    def barrier(self):
        sems = [e.sem for e in self.engs]
        for q in self.dpool:
            sems += self.dpool[q]
        for e in self.engs:
            for s in sems:
                if s is e.sem:
                    continue
                e.wait(s, s.count)

    def finish(self):
        for q in self.dpool:
            for s in self.dpool[q]:
                self.SP.wait(s, s.count)
        for e in self.engs:
            if e is not self.SP:
                self.SP.wait(e.sem, e.sem.count)


class Ring:
    def __init__(self, bufs):
        self.bufs = bufs
        self.i = 0

    def next(self):
        b = self.bufs[self.i % len(self.bufs)]
        self.i += 1
        return b


class Ctx:
    pass


def setup_common(kb, es):
    nc = kb.nc
    C = Ctx()
    C.ps = []
    C.psh = []
    for i in range(4):
        t = es.enter_context(nc.psum_tensor(f"ps{i}", [128, 1024], F32))
        C.ps.append(t)
        C.psh.append([Buf(t, {}), Buf(t, {})])
    gs = Scope(kb)
    es.callback(gs.close)
    C.gs = gs
    C.ones = gs.sb("ones", [128, 128], BF16)
    kb.op(kb.DVE, lambda: nc.vector.memset(C.ones.t[:], 1.0), writes=[C.ones])
    return C


def wslab_view(buf, kcn, ncols):
    return buf.t[:, 0:kcn * ncols].rearrange("p (k n) -> p k n", k=kcn)


def load_wslab(kb, ring, w2d, r0, kcn, c0, ncols):
    buf = ring.next()
    src = w2d[r0:r0 + kcn * 128, c0:c0 + ncols].rearrange("(k p) n -> p k n", p=128)
    kb.dma(kb.POOL, wslab_view(buf, kcn, ncols), src, writes=[buf])
    return buf


def rms_to_hT(kb, C, S, xt, gvec, hT, tag):
    nc = kb.nc
    sqr = Ring([S.sb(f"sq{tag}{i}", [128, TT], BF16) for i in range(2)])
    lnv = S.sb(f"lnv{tag}", [128, TT], F32)
    rstd = S.sb(f"rstd{tag}", [128, TT], F32)
    ssb = C.psh[2]
    sst = C.ps[2]
    for c in range(16):
        sq = sqr.next()
        xb, xap = xt[c]
        kb.op(kb.ACT, lambda: nc.scalar.activation(out=sq.t[:], in_=xap, func=AF.Square),
              reads=[xb], writes=[sq])
        for h in range(2):
            kb.op(kb.PE, lambda: nc.tensor.matmul(sst[:, h * 512:(h + 1) * 512], C.ones.t[:],
                                                  sq.t[:, h * 512:(h + 1) * 512],
                                                  start=(c == 0), stop=(c == 15)),
                  reads=[sq, C.ones], writes=[ssb[h]], inc=(h == 1))
    kb.op(kb.ACT, lambda: nc.scalar.activation(out=lnv.t[:], in_=sst[:], func=AF.Ln,
                                               scale=1.0 / D, bias=C.epsb.t[:]),
          reads=[ssb[0], ssb[1], C.epsb], writes=[lnv])
    kb.op(kb.ACT, lambda: nc.scalar.activation(out=rstd.t[:], in_=lnv.t[:], func=AF.Exp, scale=-0.5),
          reads=[lnv], writes=[rstd])
    for c in range(16):
        xb, xap = xt[c]
        kb.op(kb.DVE, lambda: nc.vector.scalar_tensor_tensor(
            out=hT[c].t[:], in0=xap, scalar=gvec.t[:, c:c + 1], in1=rstd.t[:],
            op0=ALU.mult, op1=ALU.mult), reads=[xb, gvec, rstd], writes=[hT[c]])


def load_consts_eps(kb, C):
    nc = kb.nc
    C.epsb = C.gs.sb("epsb", [128, 1], F32)
    kb.op(kb.DVE, lambda: nc.vector.memset(C.epsb.t[:], EPS), writes=[C.epsb])
    C.oneb = C.gs.sb("oneb", [128, 1], F32)
    kb.op(kb.DVE, lambda: nc.vector.memset(C.oneb.t[:], 1.0), writes=[C.oneb])


def p1_consts(kb, C, d, gs=None):
    nc = kb.nc
    gs = gs or C.gs
    C.gmix = gs.sb("gmix", [128, 16], F32)
    kb.dma(kb.SP, C.gmix.t[:], d["gmix"][:, :], writes=[C.gmix])
    C.gq = gs.sb("gq", [128, 8], F32)
    kb.dma(kb.SP, C.gq.t[:], d["gq"][:, :], writes=[C.gq])
    C.gk = gs.sb("gk", [128, 8], F32)
    kb.dma(kb.SP, C.gk.t[:], d["gk"][:, :], writes=[C.gk])
    C.gqs = gs.sb("gqs", [128, 8], F32)
    kb.op(kb.DVE, lambda: nc.vector.tensor_scalar(out=C.gqs.t[:], in0=C.gq.t[:], scalar1=float(DH ** -0.5),
                                                  scalar2=None, op0=ALU.mult),
          reads=[C.gq], writes=[C.gqs])
    C.lng = gs.sb("lng", [128, SGW], F32)
    kb.dma(kb.SP, C.lng.t[:], d["lng"][0:1, :].partition_broadcast(128), writes=[C.lng])
    C.lnb = gs.sb("lnb", [128, SGW], F32)
    kb.dma(kb.SP, C.lnb.t[:], d["lnb"][0:1, :].partition_broadcast(128), writes=[C.lnb])
    C.bsbc = gs.sb("bsbc", [128, 8 * 128], F32)
    kb.dma(kb.SP, C.bsbc.t[:], d["bsp"][0:1, :].partition_broadcast(128), writes=[C.bsbc])
    C.ident = gs.sb("ident", [128, 128], F32)
    kb.op(kb.DVE, lambda: nc.vector.memset(C.ident.t[:], 1.0), writes=[C.ident])
    kb.op(kb.POOL, lambda: nc.gpsimd.affine_select(out=C.ident.t[:], in_=C.ident.t[:], pattern=[[-1, 128]],
                                                   compare_op=ALU.is_equal, fill=0.0, base=0,
                                                   channel_multiplier=1),
          reads=[C.ident], writes=[C.ident])
    C.wsT = gs.sb("wsT", [128, 8, 128], BF16)
    wsn = gs.sb("wsn", [128, 8, 128], F32)
    kb.dma(kb.SP, wsn.t[:], d["wsp"].rearrange("g i j -> i g j"), writes=[wsn])
    for g in range(8):
        hb = C.psh[3][g // 4]
        kb.op(kb.PE, lambda: nc.tensor.transpose(C.ps[3][:, g * 128:(g + 1) * 128], wsn.t[:, g, :], C.ident.t[:]),
              reads=[wsn, C.ident], writes=[hb])
    kb.op(kb.DVE, lambda: nc.vector.tensor_copy(out=C.wsT.t[:].rearrange("p g i -> p (g i)"), in_=C.ps[3][:]),
          reads=[C.psh[3][0], C.psh[3][1]], writes=[C.wsT])
    kb.op(kb.DVE, lambda: nc.vector.memset(C.wsT.t[64:128, :, 0:64], 0.0), writes=[C.wsT])


def phase1(kb, C, d, t0):
    nc = kb.nc
    PE, ACT, DVE, SP = kb.PE, kb.ACT, kb.DVE, kb.SP
    xTv = d["xT"].rearrange("(c p) t -> p c t", p=128)
    w_in = d["w_in"]
    with kb.scope() as S0:
        hT = [S0.sb(f"hT{c}", [128, TT], BF16) for c in range(16)]
        wring = Ring([S0.sb(f"wr{i}", [128, 16 * 512], BF16) for i in range(NW)])
        with kb.scope() as S1:
            xg = [S1.sb(f"xt{i}", [128, 4, TT], F32) for i in range(4)]
            for i in range(4):
                kb.dma(SP, xg[i].t[:], xTv[:, 4 * i:4 * i + 4, t0:t0 + TT], writes=[xg[i]])
            xt = [(xg[c // 4], xg[c // 4].t[:, c % 4, :]) for c in range(16)]
            rms_to_hT(kb, C, S1, xt, C.gmix, hT, "a")
        with kb.scope() as S2:
            uT = S2.sb("uT", [128, 8, TT], BF16)
            sT = S2.sb("sT", [128, 8, TT], BF16)
            sqq = Ring([S2.sb(f"sqq{i}", [128, TT], BF16) for i in range(2)])
            lq = Ring([S2.sb(f"lq{i}", [128, TT], F32) for i in range(1)])
            rq = Ring([S2.sb(f"rq{i}", [128, TT], F32) for i in range(2)])
            ob = Ring([S2.sb(f"ob{i}", [128, TT], BF16) for i in range(3)])
            vtok = Ring([S2.sb(f"vtok{i}", [128, 1024], BF16) for i in range(2)])
            vln = Ring([S2.sb(f"vln{i}", [128, 1024], BF16) for i in range(2)])
            bst = Ring([S2.sb(f"bst{i}", [128, 6 * 2], F32) for i in range(2)])
            stmp = Ring([S2.sb(f"stmp{i}", [128, 512], F32) for i in range(2)])
            praw = Ring([0, 1])
            pending = []

            def fm_block(slab, kcn, lc, evac):
                pi = praw.next()
                pst, psb = C.ps[pi], C.psh[pi]
                for kc in range(kcn):
                    for h in range(2):
                        last = (kc == kcn - 1 and h == 1)
                        kb.op(PE, lambda: nc.tensor.matmul(
                            pst[:, h * 512:(h + 1) * 512], wslab_view(slab, kcn, 512)[:, kc, lc * 128:(lc + 1) * 128],
                            hT[kc].t[:, h * 512:(h + 1) * 512], start=(kc == 0), stop=(kc == kcn - 1)),
                            reads=[slab, hT[kc]], writes=[psb[h]], inc=last)
                while pending:
                    pending.pop(0)()
                evac(pst, psb)

            def qk_evac(row0, gvec, hd, dst):
                def ev(pst, psb):
                    sq = sqq.next()
                    kb.op(ACT, lambda: nc.scalar.activation(out=sq.t[:], in_=pst[:], func=AF.Square),
                          reads=psb, writes=[sq])

                    def later():
                        for h in range(2):
                            kb.op(PE, lambda: nc.tensor.matmul(C.ps[2][:, h * 512:(h + 1) * 512], C.ones.t[:],
                                                               sq.t[:, h * 512:(h + 1) * 512], start=True, stop=True),
                                  reads=[sq, C.ones], writes=[C.psh[2][h]], inc=(h == 1))
                        l = lq.next()
                        r = rq.next()
                        o = ob.next()
                        kb.op(ACT, lambda: nc.scalar.activation(out=l.t[:], in_=C.ps[2][:], func=AF.Ln,
                                                                scale=1.0 / DH, bias=C.epsb.t[:]),
                              reads=[C.psh[2][0], C.psh[2][1], C.epsb], writes=[l])
                        kb.op(ACT, lambda: nc.scalar.activation(out=r.t[:], in_=l.t[:], func=AF.Exp, scale=-0.5),
                              reads=[l], writes=[r])
                        kb.op(DVE, lambda: nc.vector.scalar_tensor_tensor(
                            out=o.t[:], in0=pst[:], scalar=gvec.t[:, hd:hd + 1], in1=r.t[:],
                            op0=ALU.mult, op1=ALU.mult), reads=[psb[0], psb[1], gvec, r], writes=[o])
                        kb.dma(SP, dst[row0:row0 + 128, t0:t0 + TT], o.t[:], reads=[o])
                    pending.append(later)
                return ev

            def gate_evac(row0):
                def ev(pst, psb):
                    o = ob.next()
                    kb.op(ACT, lambda: nc.scalar.activation(out=o.t[:], in_=pst[:], func=AF.Sigmoid),
                          reads=psb, writes=[o])
                    kb.dma(SP, d["gates"][row0:row0 + 128, t0:t0 + TT], o.t[:], reads=[o])
                return ev

            def u_evac(g):
                def ev(pst, psb):
                    kb.op(ACT, lambda: nc.scalar.activation(out=uT.t[:, g, :], in_=pst[:], func=AF.Gelu),
                          reads=psb, writes=[uT])
                return ev

            for s in range(4):
                slab = load_wslab(kb, wring, w_in, 0, 16, s * 512, 512)
                for lc in range(4):
                    blk = s * 4 + lc
                    if blk < 8:
                        fm_block(slab, 16, lc, qk_evac(blk * 128, C.gqs, blk, d["qT"]))
                    else:
                        fm_block(slab, 16, lc, qk_evac((blk - 8) * 128, C.gk, blk - 8, d["kT"]))
            while pending:
                pending.pop(0)()

            hring = Ring([(0, 0), (0, 1), (1, 0), (1, 1)])

            def tm_group(col0, evac):
                slabs = [load_wslab(kb, wring, w_in, 0, 16, col0 + cs * 512, 512) for cs in range(2)]
                for tb in range(TT // 128):
                    outs = []
                    for cs in range(2):
                        pi, h = hring.next()
                        pst, pb = C.ps[pi], C.psh[pi][h]
                        for kc in range(16):
                            kb.op(PE, lambda: nc.tensor.matmul(
                                pst[:, h * 512:(h + 1) * 512], hT[kc].t[:, tb * 128:(tb + 1) * 128],
                                wslab_view(slabs[cs], 16, 512)[:, kc, :], start=(kc == 0), stop=(kc == 15)),
                                reads=[slabs[cs], hT[kc]], writes=[pb], inc=(kc == 15))
                        outs.append((pst[:, h * 512:(h + 1) * 512], pb))
                    evac(tb, outs)

            def v_evac(tb, outs):
                vt = vtok.next()
                for cs in range(2):
                    ap, pb = outs[cs]
                    kb.op(DVE, lambda: nc.vector.tensor_copy(out=vt.t[:, cs * 512:(cs + 1) * 512], in_=ap),
                          reads=[pb], writes=[vt])
                kb.dma(SP, d["v"][t0 + tb * 128:t0 + (tb + 1) * 128, :], vt.t[:], reads=[vt])

            tm_group(2048, v_evac)
            for s in range(2):
                slab = load_wslab(kb, wring, w_in, 0, 16, 3072 + s * 512, 512)
                for lc in range(4):
                    fm_block(slab, 16, lc, u_evac(s * 4 + lc))

            vgall = [S2.sb(f"vga{i}", [128, 1024], F32) for i in range(TT // 128)]
            mvall = S2.sb("mvall", [128, TT // 128, 2], F32)
            lall = S2.sb("lall", [128, TT // 128], F32)
            rall = S2.sb("rall", [128, TT // 128], F32)

            def vsg_evac(tb, outs):
                g_ = vgall[tb]
                for cs in range(2):
                    ap, pb = outs[cs]
                    kb.op(ACT, lambda: nc.scalar.activation(out=g_.t[:, cs * 512:(cs + 1) * 512], in_=ap, func=AF.Gelu),
                          reads=[pb], writes=[g_])
                st = bst.next()
                for cs in range(2):
                    kb.op(DVE, lambda: nc.vector.bn_stats(out=st.t[:, cs * 6:(cs + 1) * 6],
                                                          in_=g_.t[:, cs * 512:(cs + 1) * 512]),
                          reads=[g_], writes=[st])
                kb.op(DVE, lambda: nc.vector.bn_aggr(out=mvall.t[:, tb, :], in_=st.t[:]), reads=[st], writes=[mvall])

            def sgu_tail(tb):
                g_ = vgall[tb]
                kb.op(DVE, lambda: nc.vector.tensor_scalar(out=g_.t[:], in0=g_.t[:], scalar1=mvall.t[:, tb, 0:1],
                                                           scalar2=rall.t[:, tb:tb + 1], op0=ALU.subtract, op1=ALU.mult),
                      reads=[g_, mvall, rall], writes=[g_])
                kb.op(DVE, lambda: nc.vector.tensor_tensor(out=g_.t[:], in0=g_.t[:], in1=C.lng.t[:], op=ALU.mult),
                      reads=[g_, C.lng], writes=[g_])
                vl = vln.next()
                kb.op(DVE, lambda: nc.vector.tensor_tensor(out=vl.t[:], in0=g_.t[:], in1=C.lnb.t[:], op=ALU.add),
                      reads=[g_, C.lnb], writes=[vl])
                for half in range(2):
                    pb = C.psh[3][half]
                    for gg in range(4):
                        g = half * 4 + gg
                        kb.op(PE, lambda: nc.tensor.matmul(
                            C.ps[3][:, half * 512 + gg * 128: half * 512 + (gg + 1) * 128],
                            vl.t[:, g * 128:(g + 1) * 128], C.wsT.t[:, g, :], start=True, stop=True),
                            reads=[vl, C.wsT], writes=[pb], inc=(gg == 3))
                    tmp = stmp.next()
                    kb.op(DVE, lambda: nc.vector.tensor_tensor(
                        out=tmp.t[:], in0=C.ps[3][:, half * 512:(half + 1) * 512],
                        in1=C.bsbc.t[:, half * 512:(half + 1) * 512], op=ALU.add),
                        reads=[pb, C.bsbc], writes=[tmp])
                    kb.op(DVE, lambda: nc.vector.tensor_tensor(
                        out=sT.t[:, half * 4:(half + 1) * 4, tb * 128:(tb + 1) * 128],
                        in0=tmp.t[:].rearrange("p (g i) -> p g i", g=4),
                        in1=uT.t[:, half * 4:(half + 1) * 4, tb * 128:(tb + 1) * 128], op=ALU.mult),
                        reads=[tmp, uT], writes=[sT])

            tm_group(4096, vsg_evac)
            kb.op(ACT, lambda: nc.scalar.activation(out=lall.t[:], in_=mvall.t[:, :, 1], func=AF.Ln,
                                                    bias=C.epsb.t[:]), reads=[mvall, C.epsb], writes=[lall])
            kb.op(ACT, lambda: nc.scalar.activation(out=rall.t[:], in_=lall.t[:], func=AF.Exp, scale=-0.5),
                  reads=[lall], writes=[rall])
            for tb in range(TT // 128):
                sgu_tail(tb)
            kb.dma(SP, d["sT"].rearrange("(g p) t -> p g t", p=128)[:, :, t0:t0 + TT], sT.t[:], reads=[sT])

            for s in range(8):
                slab = load_wslab(kb, wring, w_in, 0, 16, 5120 + s * 512, 512)
                for lc in range(4):
                    fm_block(slab, 16, lc, gate_evac((s * 4 + lc) * 128))


def p2_consts(kb, C):
    nc = kb.nc
    gs = C.gs
    C.negones = gs.sb("negones", [128, 128], BF16)
    kb.op(kb.DVE, lambda: nc.vector.memset(C.negones.t[:], -1.0), writes=[C.negones])
    C.negtri = gs.sb("negtri", [128, 128], BF16)
    C.masks = [gs.sb(f"mask{r}", [128, 512], BF16) for r in range(4)]
    with kb.scope() as ts:
        onesf = ts.sb("onesf", [128, 512], F32)
        kb.op(kb.DVE, lambda: nc.vector.memset(onesf.t[:], 1.0), writes=[onesf])
        mone = ts.sb("monef", [128, 128], F32)
        kb.op(kb.DVE, lambda: nc.vector.memset(mone.t[:], -1.0), writes=[mone])
        trif = ts.sb("trif", [128, 128], F32)
        kb.op(kb.POOL, lambda: nc.gpsimd.affine_select(out=trif.t[:], in_=mone.t[:], pattern=[[-1, 128]],
                                                       compare_op=ALU.is_ge, fill=0.0, base=0, channel_multiplier=1),
              reads=[mone], writes=[trif])
        kb.op(kb.DVE, lambda: nc.vector.tensor_copy(out=C.negtri.t[:], in_=trif.t[:]), reads=[trif], writes=[C.negtri])
        mf = ts.sb("maskf", [128, 512], F32)
        for r in range(4):
            kb.op(kb.POOL, lambda: nc.gpsimd.affine_select(out=mf.t[:], in_=onesf.t[:], pattern=[[1, 512]],
                                                           compare_op=ALU.is_gt, fill=0.0, base=-r * 128,
                                                           channel_multiplier=-1),
                  reads=[onesf], writes=[mf])
            kb.op(kb.DVE, lambda: nc.vector.tensor_copy(out=C.masks[r].t[:], in_=mf.t[:]), reads=[mf],
                  writes=[C.masks[r]])


def phase2(kb, C, d, S, nheads):
    nc = kb.nc
    PE, ACT, DVE, SP = kb.PE, kb.ACT, kb.DVE, kb.SP
    NKB = S // 128
    NQC = S // 512
    with kb.scope() as S0:
        hb = []
        for i in range(2):
            hb.append(dict(q=S0.sb(f"qh{i}", [128, S], BF16), k=S0.sb(f"kh{i}", [128, S], BF16),
                           v=S0.sb(f"vh{i}", [128, NKB, 128], BF16), o=S0.sb(f"oh{i}", [128, S], BF16)))
        er = Ring([S0.sb(f"e{i}", [128, 512], F32) for i in range(2)])
        spr = Ring([S0.sb(f"sp{i}", [128, 512], BF16) for i in range(4)])
        sar = Ring([S0.sb(f"sa{i}", [128, 512], BF16) for i in range(3)])
        ar = Ring([S0.sb(f"a{i}", [128, 512], BF16) for i in range(3)])
        tiles = []
        for hd in range(nheads):
            for qc in range(NQC):
                n = 4 * qc + 4
                for i in range(n):
                    tiles.append(dict(hd=hd, qc=qc, i=i, n=n, kb=n - 1 - i, r=(3 - i) if i < 4 else None))
        state = {}

        def load_head(hd):
            B = hb[hd % 2]
            kb.dma(SP, B["q"].t[:], d["qh"][hd], writes=[B["q"]])
            kb.dma(SP, B["k"].t[:], d["kh"][hd], writes=[B["k"]])
            kb.dma(SP, B["v"].t[:], d["vh"][hd].rearrange("(b p) x -> p b x", p=128), writes=[B["v"]])

        def stageA(idx):
            t = tiles[idx]
            B = hb[t["hd"] % 2]
            if t["qc"] == 0 and t["i"] == 0:
                load_head(t["hd"])
            q0 = t["qc"] * 512
            k0 = t["kb"] * 128
            zi = idx % 2
            zb = C.psh[0][zi]
            zap = C.ps[0][:, zi * 512:(zi + 1) * 512]
            kb.op(PE, lambda: nc.tensor.matmul(zap, B["k"].t[:, k0:k0 + 128], B["q"].t[:, q0:q0 + 512],
                                               start=True, stop=True),
                  reads=[B["k"], B["q"]], writes=[zb])
            e = er.next()
            kb.op(ACT, lambda: nc.scalar.activation(out=e.t[:], in_=zap, func=AF.Exp), reads=[zb], writes=[e])
            sp = spr.next()
            kb.op(ACT, lambda: nc.scalar.activation(out=sp.t[:], in_=e.t[:], func=AF.Ln, bias=C.oneb.t[:]),
                  reads=[e, C.oneb], writes=[sp])
            if t["r"] is not None:
                m = C.masks[t["r"]]
                kb.op(DVE, lambda: nc.vector.tensor_tensor(out=sp.t[:], in0=sp.t[:], in1=m.t[:], op=ALU.mult),
                      reads=[sp, m], writes=[sp])
            t["sp"] = sp
            if t["i"] == 0:
                t["sacc"] = None
                state["next_sacc"] = sp
            else:
                t["sacc"] = state["next_sacc"]
                if t["i"] < t["n"] - 1:
                    ns = sar.next()
                    prev = state["next_sacc"]
                    kb.op(DVE, lambda: nc.vector.tensor_tensor(out=ns.t[:], in0=prev.t[:], in1=sp.t[:], op=ALU.add),
                          reads=[prev, sp], writes=[ns])
                    state["next_sacc"] = ns

        def stageB(idx):
            t = tiles[idx]
            B = hb[t["hd"] % 2]
            q0 = t["qc"] * 512
            k0 = t["kb"] * 128
            li = idx % 2
            lb = C.psh[1][li]
            lap = C.ps[1][:, li * 512:(li + 1) * 512]
            has_s = t["sacc"] is not None
            kb.op(PE, lambda: nc.tensor.matmul(lap, B["k"].t[:, k0:k0 + 128], B["q"].t[:, q0:q0 + 512],
                                               start=True, stop=False),
                  reads=[B["k"], B["q"]], writes=[lb], inc=False)
            kb.op(PE, lambda: nc.tensor.matmul(lap, C.negtri.t[:], t["sp"].t[:], start=False, stop=not has_s),
                  reads=[C.negtri, t["sp"]], writes=[lb], inc=not has_s)
            if has_s:
                kb.op(PE, lambda: nc.tensor.matmul(lap, C.negones.t[:], t["sacc"].t[:], start=False, stop=True),
                      reads=[C.negones, t["sacc"]], writes=[lb])
            a = ar.next()
            kb.op(ACT, lambda: nc.scalar.activation(out=a.t[:], in_=lap, func=AF.Exp), reads=[lb], writes=[a])
            if t["r"] is not None:
                m = C.masks[t["r"]]
                kb.op(DVE, lambda: nc.vector.tensor_tensor(out=a.t[:], in0=a.t[:], in1=m.t[:], op=ALU.mult),
                      reads=[a, m], writes=[a])
            t["a"] = a

        def stageC(idx):
            t = tiles[idx]
            B = hb[t["hd"] % 2]
            q0 = t["qc"] * 512
            oi = t["qc"] % 2
            obuf = C.psh[2][oi]
            oap = C.ps[2][:, oi * 512:(oi + 1) * 512]
            last = (t["i"] == t["n"] - 1)
            kb.op(PE, lambda: nc.tensor.matmul(oap, B["v"].t[:, t["kb"], :], t["a"].t[:],
                                               start=(t["i"] == 0), stop=last),
                  reads=[B["v"], t["a"]], writes=[obuf])
            if last:
                kb.op(DVE, lambda: nc.vector.tensor_copy(out=B["o"].t[:, q0:q0 + 512], in_=oap),
                      reads=[obuf], writes=[B["o"]])
                if t["qc"] == NQC - 1:
                    kb.dma(SP, d["oh"][t["hd"]], B["o"].t[:], reads=[B["o"]])

        N = len(tiles)
        for step in range(N + 2):
            if step < N:
                stageA(step)
            if 0 <= step - 1 < N:
                stageB(step - 1)
            if 0 <= step - 2 < N:
                stageC(step - 2)


def p3_consts(kb, C, d, gs=None):
    C.gff = (gs or C.gs).sb("gff", [128, 16], F32)
    kb.dma(kb.SP, C.gff.t[:], d["gff"][:, :], writes=[C.gff])


def phase3(kb, C, d, t0):
    nc = kb.nc
    PE, ACT, DVE, SP = kb.PE, kb.ACT, kb.DVE, kb.SP
    xin = d["xT"].rearrange("(c p) t -> p c t", p=128)
    xout = d["xo"].rearrange("(c p) t -> p c t", p=128)
    pr = Ring([0, 1, 2, 3])
    with kb.scope() as S0:
        xg = [S0.sb(f"x3_{i}", [128, 4, TT], F32) for i in range(4)]
        for i in range(4):
            kb.dma(SP, xg[i].t[:], xin[:, 4 * i:4 * i + 4, t0:t0 + TT], writes=[xg[i]])
        xt = [(xg[c // 4], xg[c // 4].t[:, c % 4, :]) for c in range(16)]
        wring = Ring([S0.sb(f"w3r{i}", [128, 16 * 512], BF16) for i in range(NW)])
        with kb.scope() as S1:
            oT = S1.sb("oT", [128, 8, TT], BF16)
            sT = S1.sb("sT3", [128, 8, TT], BF16)
            kb.dma(SP, oT.t[:], d["oT"].rearrange("(g p) t -> p g t", p=128)[:, :, t0:t0 + TT], writes=[oT])
            kb.dma(SP, sT.t[:], d["sT"].rearrange("(g p) t -> p g t", p=128)[:, :, t0:t0 + TT], writes=[sT])
            mg = [S1.sb(f"mg{c}", [128, TT], BF16) for c in range(16)]
            gar = Ring([S1.sb(f"ga{i}", [128, TT], BF16) for i in range(2)])
            gbr = Ring([S1.sb(f"gb{i}", [128, TT], BF16) for i in range(2)])
            m1r = Ring([S1.sb(f"m1{i}", [128, TT], F32) for i in range(2)])
            m2r = Ring([S1.sb(f"m2{i}", [128, TT], F32) for i in range(2)])
            for half in range(2):
                sa = load_wslab(kb, wring, d["w_oa"], 0, 8, half * 1024, 1024)
                sbb = load_wslab(kb, wring, d["w_ob"], 0, 8, half * 1024, 1024)
                for lc in range(8):
                    f = half * 8 + lc
                    pa, pb_ = pr.next(), pr.next()
                    for (pi, slab, act) in ((pa, sa, oT), (pb_, sbb, sT)):
                        for kc in range(8):
                            for h in range(2):
                                kb.op(PE, lambda: nc.tensor.matmul(
                                    C.ps[pi][:, h * 512:(h + 1) * 512],
                                    wslab_view(slab, 8, 1024)[:, kc, lc * 128:(lc + 1) * 128],
                                    act.t[:, kc, h * 512:(h + 1) * 512], start=(kc == 0), stop=(kc == 7)),
                                    reads=[slab, act], writes=[C.psh[pi][h]], inc=(kc == 7 and h == 1))
                    ga, gb = gar.next(), gbr.next()
                    kb.dma(SP, ga.t[:], d["gates"][f * 128:(f + 1) * 128, t0:t0 + TT], writes=[ga])
                    kb.dma(SP, gb.t[:], d["gates"][D + f * 128:D + (f + 1) * 128, t0:t0 + TT], writes=[gb])
                    m1, m2 = m1r.next(), m2r.next()
                    kb.op(DVE, lambda: nc.vector.tensor_tensor(out=m1.t[:], in0=C.ps[pa][:], in1=ga.t[:], op=ALU.mult),
                          reads=[C.psh[pa][0], C.psh[pa][1], ga], writes=[m1])
                    kb.op(DVE, lambda: nc.vector.tensor_tensor(out=m2.t[:], in0=C.ps[pb_][:], in1=gb.t[:], op=ALU.mult),
                          reads=[C.psh[pb_][0], C.psh[pb_][1], gb], writes=[m2])
                    kb.op(kb.POOL, lambda: nc.gpsimd.tensor_tensor(out=mg[f].t[:], in0=m1.t[:], in1=m2.t[:], op=ALU.add),
                          reads=[m1, m2], writes=[mg[f]])
            for s in range(4):
                slab = load_wslab(kb, wring, d["w_out"], 0, 16, s * 512, 512)
                for lc in range(4):
                    f = s * 4 + lc
                    pi = pr.next()
                    for kc in range(16):
                        for h in range(2):
                            kb.op(PE, lambda: nc.tensor.matmul(
                                C.ps[pi][:, h * 512:(h + 1) * 512],
                                wslab_view(slab, 16, 512)[:, kc, lc * 128:(lc + 1) * 128],
                                mg[kc].t[:, h * 512:(h + 1) * 512], start=(kc == 0), stop=(kc == 15)),
                                reads=[slab, mg[kc]], writes=[C.psh[pi][h]], inc=(kc == 15 and h == 1))
                    xb, xap = xt[f]
                    kb.op(DVE, lambda: nc.vector.tensor_tensor(out=xap, in0=xap, in1=C.ps[pi][:], op=ALU.add),
                          reads=[xb, C.psh[pi][0], C.psh[pi][1]], writes=[xb])
        with kb.scope() as S2:
            hT = [S2.sb(f"h2T{c}", [128, TT], BF16) for c in range(16)]
            rms_to_hT(kb, C, S2, xt, C.gff, hT, "f")
            NG = 16
            hid = [[S2.sb(f"hid{i}_{j}", [128, TT], BF16) for j in range(4)] for i in range(2)]
            rr = Ring([S2.sb(f"rr{i}", [128, TT], F32) for i in range(2)])

            def F1(g):
                slab = load_wslab(kb, wring, d["w_ff1"], 0, 16, g * 512, 512)
                for j in range(4):
                    pi = pr.next()
                    for kc in range(16):
                        for h in range(2):
                            kb.op(PE, lambda: nc.tensor.matmul(
                                C.ps[pi][:, h * 512:(h + 1) * 512],
                                wslab_view(slab, 16, 512)[:, kc, j * 128:(j + 1) * 128],
                                hT[kc].t[:, h * 512:(h + 1) * 512], start=(kc == 0), stop=(kc == 15)),
                                reads=[slab, hT[kc]], writes=[C.psh[pi][h]], inc=(kc == 15 and h == 1))
                    r = rr.next()
                    hb_ = hid[g % 2][j]
                    kb.op(ACT, lambda: nc.scalar.activation(out=r.t[:], in_=C.ps[pi][:], func=AF.Relu),
                          reads=[C.psh[pi][0], C.psh[pi][1]], writes=[r])
                    kb.op(kb.POOL, lambda: nc.gpsimd.tensor_tensor(out=hb_.t[:], in0=r.t[:], in1=r.t[:], op=ALU.mult),
                          reads=[r], writes=[hb_])

            def F2(g):
                slab = load_wslab(kb, wring, d["w_ff2"], g * 512, 4, 0, 2048)
                for o in range(16):
                    pi = pr.next()
                    for j in range(4):
                        for h in range(2):
                            kb.op(PE, lambda: nc.tensor.matmul(
                                C.ps[pi][:, h * 512:(h + 1) * 512],
                                wslab_view(slab, 4, 2048)[:, j, o * 128:(o + 1) * 128],
                                hid[g % 2][j].t[:, h * 512:(h + 1) * 512], start=(j == 0), stop=(j == 3)),
                                reads=[slab, hid[g % 2][j]], writes=[C.psh[pi][h]], inc=(j == 3 and h == 1))
                    xb, xap = xt[o]
                    kb.op(DVE, lambda: nc.vector.tensor_tensor(out=xap, in0=xap, in1=C.ps[pi][:], op=ALU.add),
                          reads=[xb, C.psh[pi][0], C.psh[pi][1]], writes=[xb])

            for g in range(NG + 1):
                if g < NG:
                    F1(g)
                if g >= 1:
                    F2(g - 1)
        for i in range(4):
            kb.dma(SP, xout[:, 4 * i:4 * i + 4, t0:t0 + TT], xg[i].t[:], reads=[xg[i]])


def _in(nc, name, shape, dt):
    return nc.dram_tensor(name, list(shape), dt, kind="ExternalInput").ap()


def _out(nc, name, shape, dt):
    return nc.dram_tensor(name, list(shape), dt, kind="ExternalOutput").ap()


def build_p1(T):
    nc = bass.Bass("TRN2", target_bir_lowering=False)
    d = dict(
        xT=_in(nc, "xT", [D, T], F32), gmix=_in(nc, "gmix", [128, 16], F32), w_in=_in(nc, "w_in", [D, INC], F32),
        gq=_in(nc, "gq", [128, 8], F32), gk=_in(nc, "gk", [128, 8], F32),
        lng=_in(nc, "lng", [1, SGW], F32), lnb=_in(nc, "lnb", [1, SGW], F32),
        wsp=_in(nc, "wsp", [8, 128, 128], F32), bsp=_in(nc, "bsp", [1, 8 * 128], F32),
        qT=_out(nc, "qT", [SBW, T], BF16), kT=_out(nc, "kT", [SBW, T], BF16), v=_out(nc, "v", [T, SBW], BF16),
        sT=_out(nc, "sT", [SGW, T], BF16), gates=_out(nc, "gates", [2 * D, T], BF16),
    )
    with ExitStack() as es:
        kb = KB(nc)
        C = setup_common(kb, es)
        load_consts_eps(kb, C)
        p1_consts(kb, C, d)
        for t0 in range(0, T, TT):
            phase1(kb, C, d, t0)
        kb.finish()
    return nc


def build_p2(S, nheads=2):
    nc = bass.Bass("TRN2", target_bir_lowering=False)
    d = dict(qh=_in(nc, "qh", [nheads, 128, S], BF16), kh=_in(nc, "kh", [nheads, 128, S], BF16),
             vh=_in(nc, "vh", [nheads, S, 128], BF16), oh=_out(nc, "oh", [nheads, 128, S], BF16))
    with ExitStack() as es:
        kb = KB(nc)
        C = setup_common(kb, es)
        load_consts_eps(kb, C)
        p2_consts(kb, C)
        phase2(kb, C, d, S, nheads)
        kb.finish()
    return nc


def build_p3(T):
    nc = bass.Bass("TRN2", target_bir_lowering=False)
    d = dict(
        xT=_in(nc, "xT", [D, T], F32), oT=_in(nc, "oT", [SBW, T], BF16), sT=_in(nc, "sT", [SGW, T], BF16),
        gates=_in(nc, "gates", [2 * D, T], BF16), w_oa=_in(nc, "w_oa", [SBW, D], F32),
        w_ob=_in(nc, "w_ob", [SGW, D], F32), w_out=_in(nc, "w_out", [D, D], F32), gff=_in(nc, "gff", [128, 16], F32),
        w_ff1=_in(nc, "w_ff1", [D, DFF], F32), w_ff2=_in(nc, "w_ff2", [DFF, D], F32),
        xo=_out(nc, "xo", [D, T], F32),
    )
    with ExitStack() as es:
        kb = KB(nc)
        C = setup_common(kb, es)
        load_consts_eps(kb, C)
        p3_consts(kb, C, d)
        for t0 in range(0, T, TT):
            phase3(kb, C, d, t0)
        kb.finish()
    return nc


def build_fused(S, depth):
    nc = bass.Bass("TRN2", target_bir_lowering=False)
    x_in = _in(nc, "xT", [D, S], F32)
    W = dict(
        gmix=_in(nc, "gmix", [depth, 128, 16], F32), w_in=_in(nc, "w_in", [depth, D, INC], F32),
        gq=_in(nc, "gq", [depth, 128, 8], F32), gk=_in(nc, "gk", [depth, 128, 8], F32),
        lng=_in(nc, "lng", [depth, 1, SGW], F32), lnb=_in(nc, "lnb", [depth, 1, SGW], F32),
        wsp=_in(nc, "wsp", [depth, 8, 128, 128], F32), bsp=_in(nc, "bsp", [depth, 1, 8 * 128], F32),
        w_oa=_in(nc, "w_oa", [depth, SBW, D], F32), w_ob=_in(nc, "w_ob", [depth, SGW, D], F32),
        w_out=_in(nc, "w_out", [depth, D, D], F32), gff=_in(nc, "gff", [depth, 128, 16], F32),
        w_ff1=_in(nc, "w_ff1", [depth, D, DFF], F32), w_ff2=_in(nc, "w_ff2", [depth, DFF, D], F32))
    y_out = _out(nc, "yT", [D, S], F32)
    xbuf = nc.dram_tensor("xbuf", [D, S], F32).ap()
    qT = nc.dram_tensor("qT_s", [SBW, S], BF16).ap()
    kT = nc.dram_tensor("kT_s", [SBW, S], BF16).ap()
    v = nc.dram_tensor("v_s", [S, SBW], BF16).ap()
    sT = nc.dram_tensor("sT_s", [SGW, S], BF16).ap()
    gates = nc.dram_tensor("gates_s", [2 * D, S], BF16).ap()
    oT = nc.dram_tensor("oT_s", [SBW, S], BF16).ap()
    with ExitStack() as es:
        kb = KB(nc)
        C = setup_common(kb, es)
        load_consts_eps(kb, C)
        p2_consts(kb, C)
        for l in range(depth):
            src = x_in if l == 0 else xbuf
            dst = y_out if l == depth - 1 else xbuf
            d = {k: a[l] for k, a in W.items()}
            d.update(xT=src, xo=dst, qT=qT, kT=kT, v=v, sT=sT, gates=gates, oT=oT)
            d2 = dict(qh=qT.rearrange("(h p) s -> h p s", p=128), kh=kT.rearrange("(h p) s -> h p s", p=128),
                      vh=v.rearrange("s (h x) -> h s x", h=NH), oh=oT.rearrange("(h p) s -> h p s", p=128))
            with kb.scope() as LS:
                p1_consts(kb, C, d, LS)
                for t0 in range(0, S, TT):
                    phase1(kb, C, d, t0)
            kb.barrier()
            phase2(kb, C, d2, S, NH)
            kb.barrier()
            with kb.scope() as LS:
                p3_consts(kb, C, d, LS)
                for t0 in range(0, S, TT):
                    phase3(kb, C, d, t0)
            kb.barrier()
        kb.finish()
    return nc


def kernel_fused(x, g_mix, w_in, g_q, g_k, sgu_ln_g, sgu_ln_b, w_spatial, b_spatial, w_oa, w_ob, w_out, g_ff,
                 w_ff1, w_ff2):
    x = np.asarray(x)
    B, S, _ = x.shape
    f32 = np.float32
    depth = np.asarray(w_in).shape[0]
    gpb = NCORES // B
    cores = list(range(NCORES))
    shared = dict(
        gmix=_c(np.asarray(g_mix, f32).reshape(depth, 16, 128).transpose(0, 2, 1)), w_in=_c(np.asarray(w_in, f32)),
        gq=_c(np.asarray(g_q, f32).transpose(0, 2, 1)), gk=_c(np.asarray(g_k, f32).transpose(0, 2, 1)),
        lng=_c(np.asarray(sgu_ln_g, f32).reshape(depth, 1, SGW)), lnb=_c(np.asarray(sgu_ln_b, f32).reshape(depth, 1, SGW)),
        wsp=_c(np.asarray(w_spatial, f32)), bsp=_c(np.asarray(b_spatial, f32).reshape(depth, 1, 8 * 128)),
        w_oa=_c(np.asarray(w_oa, f32)), w_ob=_c(np.asarray(w_ob, f32)), w_out=_c(np.asarray(w_out, f32)),
        gff=_c(np.asarray(g_ff, f32).reshape(depth, 16, 128).transpose(0, 2, 1)),
        w_ff1=_c(np.asarray(w_ff1, f32)), w_ff2=_c(np.asarray(w_ff2, f32)))
    xTs = [_c(x[b].T) for b in range(B)]
    nc = _prog(("fused", S, depth), lambda: build_fused(S, depth))
    res = run_bass_kernel_spmd(nc, [dict(shared, xT=xTs[c // gpb]) for c in cores], core_ids=cores).results
    out = np.empty((B, S, D), dtype=np.float32)
    for b in range(B):
        out[b] = res[b * gpb]["yT"].T
    return out


_PROGS = {}


def _prog(key, fn):
    if key not in _PROGS:
        _PROGS[key] = fn()
    return _PROGS[key]


def _c(a):
    return np.ascontiguousarray(a)


def kernel_unfused(x, g_mix, w_in, g_q, g_k, sgu_ln_g, sgu_ln_b, w_spatial, b_spatial, w_oa, w_ob, w_out, g_ff, w_ff1,
           w_ff2):
    x = np.asarray(x)
    B, S, _ = x.shape
    depth = np.asarray(w_in).shape[0]
    gpb = NCORES // B
    T = S // gpb
    hpc = NH // gpb
    cores = list(range(NCORES))
    xT = [_c(x[c // gpb, (c % gpb) * T:(c % gpb + 1) * T, :].T) for c in cores]
    f32 = np.float32
    for l in range(depth):
        wl = _c(np.asarray(w_in[l], dtype=f32))
        common1 = dict(
            gmix=_c(np.asarray(g_mix[l], f32).reshape(16, 128).T), w_in=wl,
            gq=_c(np.asarray(g_q[l], f32).T), gk=_c(np.asarray(g_k[l], f32).T),
            lng=_c(np.asarray(sgu_ln_g[l], f32).reshape(1, SGW)), lnb=_c(np.asarray(sgu_ln_b[l], f32).reshape(1, SGW)),
            wsp=_c(np.asarray(w_spatial[l], f32)), bsp=_c(np.asarray(b_spatial[l], f32).reshape(1, 8 * 128)))
        r1 = run_bass_kernel_spmd(_prog(("p1", T), lambda: build_p1(T)),
                                  [dict(common1, xT=xT[c]) for c in cores], core_ids=cores).results
        in2 = []
        for c in cores:
            b, hp = c // gpb, c % gpb
            src = [r1[b * gpb + j] for j in range(gpb)]
            rows = slice(hp * hpc * 128, (hp + 1) * hpc * 128)
            qh = np.concatenate([s_["qT"][rows] for s_ in src], axis=1).reshape(hpc, 128, S)
            kh = np.concatenate([s_["kT"][rows] for s_ in src], axis=1).reshape(hpc, 128, S)
            vh = np.concatenate([s_["v"][:, rows] for s_ in src], axis=0).reshape(S, hpc, 128).transpose(1, 0, 2)
            in2.append(dict(qh=_c(qh), kh=_c(kh), vh=_c(vh)))
        r2 = run_bass_kernel_spmd(_prog(("p2", S, hpc), lambda: build_p2(S, hpc)), in2, core_ids=cores).results
        common3 = dict(w_oa=_c(np.asarray(w_oa[l], f32)), w_ob=_c(np.asarray(w_ob[l], f32)),
                       w_out=_c(np.asarray(w_out[l], f32)), gff=_c(np.asarray(g_ff[l], f32).reshape(16, 128).T),
                       w_ff1=_c(np.asarray(w_ff1[l], f32)), w_ff2=_c(np.asarray(w_ff2[l], f32)))
        in3 = []
        for c in cores:
            b, j = c // gpb, c % gpb
            oT = np.concatenate([r2[b * gpb + hp]["oh"][:, :, j * T:(j + 1) * T].reshape(hpc * 128, T)
                                 for hp in range(gpb)], axis=0)
            in3.append(dict(common3, xT=xT[c], oT=_c(oT), sT=r1[c]["sT"], gates=r1[c]["gates"]))
        r3 = run_bass_kernel_spmd(_prog(("p3", T), lambda: build_p3(T)), in3, core_ids=cores).results
        xT = [r3[c]["xo"] for c in cores]
    out = np.empty((B, S, D), dtype=np.float32)
    for c in cores:
        out[c // gpb, (c % gpb) * T:(c % gpb + 1) * T, :] = xT[c].T
    return out


FUSED = True


def kernel(**inputs):
    return kernel_fused(**inputs) if FUSED else kernel_unfused(**inputs)
```

```python
import numpy as np
from contextlib import ExitStack, contextmanager
import ml_dtypes

import concourse.bass as bass
import concourse.mybir as mybir
from concourse.bass_utils import run_bass_kernel_spmd

F32 = mybir.dt.float32
BF16 = mybir.dt.bfloat16
AF = mybir.ActivationFunctionType
ALU = mybir.AluOpType

D = 2048
NH = 8
DH = 128
SBW = 1024
SGW = 1024
DFF = 8192
INC = 9216
EPS = 1e-6
TT = 1024
NCORES = 8
NW = 3


class Sem:
    def __init__(self, nc, name):
        self.h = nc.alloc_semaphore(name=name)
        self.count = 0


class Eng:
    def __init__(self, nc, name, e):
        self.e = e
        self.name = name
        self.sem = Sem(nc, "prog_" + name)
        self.seen = {}

    def wait(self, sem, val):
        if val <= 0 or self.seen.get(sem, 0) >= val:
            return
        self.e.wait_ge(sem.h, val)
        self.seen[sem] = val


class Buf:
    def __init__(self, t, fence):
        self.t = t
        self.w = None
        self.r = dict(fence)


class Scope:
    def __init__(self, kb):
        self.kb = kb
        self.es = ExitStack()
        self.bufs = []

    def sb(self, name, shape, dtype):
        kb = self.kb
        kb.uid += 1
        t = self.es.enter_context(kb.nc.sbuf_tensor(f"{name}_{kb.uid}", list(shape), dtype))
        b = Buf(t, kb.fence)
        self.bufs.append(b)
        return b

    def close(self):
        kb = self.kb
        for b in self.bufs:
            if b.w is not None:
                kb.fence[b.w[0]] = max(kb.fence.get(b.w[0], 0), b.w[1])
            for s, v in b.r.items():
                kb.fence[s] = max(kb.fence.get(s, 0), v)
        self.es.close()


class KB:
    def __init__(self, nc):
        self.nc = nc
        self.uid = 0
        self.fence = {}
        self.PE = Eng(nc, "pe", nc.tensor)
        self.ACT = Eng(nc, "act", nc.scalar)
        self.DVE = Eng(nc, "dve", nc.vector)
        self.POOL = Eng(nc, "pool", nc.gpsimd)
        self.SP = Eng(nc, "sp", nc.sync)
        self.engs = [self.PE, self.ACT, self.DVE, self.POOL, self.SP]
        self.dpool = {}
        self.dpi = {}
        for q in (self.SP, self.POOL, self.ACT):
            self.dpool[q] = [Sem(nc, f"dma_{q.name}_{i}") for i in range(8)]
            self.dpi[q] = 0
        self.gbufs = []

    @contextmanager
    def scope(self):
        s = Scope(self)
        try:
            yield s
        finally:
            s.close()

    def _deps(self, eng, reads, writes):
        for b in reads:
            if b.w is not None:
                self._w(eng, b.w)
        for b in writes:
            if b.w is not None:
                self._w(eng, b.w)
            for s, v in b.r.items():
                self._w(eng, (s, v))

    def _w(self, eng, st):
        sem, val = st
        if sem is eng.sem and eng is self.PE:
            return
        eng.wait(sem, val)

    def _stamp(self, st, reads, writes):
        for b in reads:
            b.r[st[0]] = max(b.r.get(st[0], 0), st[1])
        for b in writes:
            b.w = st
            b.r = {}

    def op(self, eng, fn, reads=(), writes=(), inc=True):
        self._deps(eng, reads, writes)
        ins = fn()
        if inc:
            eng.sem.count += 1
            ins.then_inc(eng.sem.h, 1)
            st = (eng.sem, eng.sem.count)
        else:
            st = (eng.sem, eng.sem.count + 1)
        self._stamp(st, reads, writes)

    def dma(self, q, out_ap, in_ap, reads=(), writes=()):
        self._deps(q, reads, writes)
        pool = self.dpool[q]
        sem = pool[self.dpi[q] % len(pool)]
        self.dpi[q] += 1
        q.wait(sem, sem.count)
        ins = q.e.dma_start(out=out_ap, in_=in_ap)
        sem.count += 16
        ins.then_inc(sem.h, 16)
        self._stamp((sem, sem.count), reads, writes)

    def barrier(self):
        sems = [e.sem for e in self.engs]
        for q in self.dpool:
            sems += self.dpool[q]
        for e in self.engs:
            for s in sems:
                if s is e.sem:
                    continue
                e.wait(s, s.count)

    def finish(self):
        for q in self.dpool:
            for s in self.dpool[q]:
                self.SP.wait(s, s.count)
        for e in self.engs:
            if e is not self.SP:
                self.SP.wait(e.sem, e.sem.count)


class Ring:
    def __init__(self, bufs):
        self.bufs = bufs
        self.i = 0

    def next(self):
        b = self.bufs[self.i % len(self.bufs)]
        self.i += 1
        return b


class Ctx:
    pass


def setup_common(kb, es):
    nc = kb.nc
    C = Ctx()
    C.ps = []
    C.psh = []
    for i in range(4):
        t = es.enter_context(nc.psum_tensor(f"ps{i}", [128, 1024], F32))
        C.ps.append(t)
        C.psh.append([Buf(t, {}), Buf(t, {})])
    gs = Scope(kb)
    es.callback(gs.close)
    C.gs = gs
    C.ones = gs.sb("ones", [128, 128], BF16)
    kb.op(kb.DVE, lambda: nc.vector.memset(C.ones.t[:], 1.0), writes=[C.ones])
    return C


def wslab_view(buf, kcn, ncols):
    return buf.t[:, 0:kcn * ncols].rearrange("p (k n) -> p k n", k=kcn)


def load_wslab(kb, ring, w2d, r0, kcn, c0, ncols):
    buf = ring.next()
    src = w2d[r0:r0 + kcn * 128, c0:c0 + ncols].rearrange("(k p) n -> p k n", p=128)
    kb.dma(kb.POOL, wslab_view(buf, kcn, ncols), src, writes=[buf])
    return buf


def rms_to_hT(kb, C, S, xt, gvec, hT, tag):
    nc = kb.nc
    sqr = Ring([S.sb(f"sq{tag}{i}", [128, TT], BF16) for i in range(2)])
    lnv = S.sb(f"lnv{tag}", [128, TT], F32)
    rstd = S.sb(f"rstd{tag}", [128, TT], F32)
    ssb = C.psh[2]
    sst = C.ps[2]
    for c in range(16):
        sq = sqr.next()
        xb, xap = xt[c]
        kb.op(kb.ACT, lambda: nc.scalar.activation(out=sq.t[:], in_=xap, func=AF.Square),
              reads=[xb], writes=[sq])
        for h in range(2):
            kb.op(kb.PE, lambda: nc.tensor.matmul(sst[:, h * 512:(h + 1) * 512], C.ones.t[:],
                                                  sq.t[:, h * 512:(h + 1) * 512],
                                                  start=(c == 0), stop=(c == 15)),
                  reads=[sq, C.ones], writes=[ssb[h]], inc=(h == 1))
    kb.op(kb.ACT, lambda: nc.scalar.activation(out=lnv.t[:], in_=sst[:], func=AF.Ln,
                                               scale=1.0 / D, bias=C.epsb.t[:]),
          reads=[ssb[0], ssb[1], C.epsb], writes=[lnv])
    kb.op(kb.ACT, lambda: nc.scalar.activation(out=rstd.t[:], in_=lnv.t[:], func=AF.Exp, scale=-0.5),
          reads=[lnv], writes=[rstd])
    for c in range(16):
        xb, xap = xt[c]
        kb.op(kb.DVE, lambda: nc.vector.scalar_tensor_tensor(
            out=hT[c].t[:], in0=xap, scalar=gvec.t[:, c:c + 1], in1=rstd.t[:],
            op0=ALU.mult, op1=ALU.mult), reads=[xb, gvec, rstd], writes=[hT[c]])


def load_consts_eps(kb, C):
    nc = kb.nc
    C.epsb = C.gs.sb("epsb", [128, 1], F32)
    kb.op(kb.DVE, lambda: nc.vector.memset(C.epsb.t[:], EPS), writes=[C.epsb])
    C.oneb = C.gs.sb("oneb", [128, 1], F32)
    kb.op(kb.DVE, lambda: nc.vector.memset(C.oneb.t[:], 1.0), writes=[C.oneb])


def p1_consts(kb, C, d, gs=None):
    nc = kb.nc
    gs = gs or C.gs
    C.gmix = gs.sb("gmix", [128, 16], F32)
    kb.dma(kb.SP, C.gmix.t[:], d["gmix"][:, :], writes=[C.gmix])
    C.gq = gs.sb("gq", [128, 8], F32)
    kb.dma(kb.SP, C.gq.t[:], d["gq"][:, :], writes=[C.gq])
    C.gk = gs.sb("gk", [128, 8], F32)
    kb.dma(kb.SP, C.gk.t[:], d["gk"][:, :], writes=[C.gk])
    C.gqs = gs.sb("gqs", [128, 8], F32)
    kb.op(kb.DVE, lambda: nc.vector.tensor_scalar(out=C.gqs.t[:], in0=C.gq.t[:], scalar1=float(DH ** -0.5),
                                                  scalar2=None, op0=ALU.mult),
          reads=[C.gq], writes=[C.gqs])
    C.lng = gs.sb("lng", [128, SGW], F32)
    kb.dma(kb.SP, C.lng.t[:], d["lng"][0:1, :].partition_broadcast(128), writes=[C.lng])
    C.lnb = gs.sb("lnb", [128, SGW], F32)
    kb.dma(kb.SP, C.lnb.t[:], d["lnb"][0:1, :].partition_broadcast(128), writes=[C.lnb])
    C.bsbc = gs.sb("bsbc", [128, 8 * 128], F32)
    kb.dma(kb.SP, C.bsbc.t[:], d["bsp"][0:1, :].partition_broadcast(128), writes=[C.bsbc])
    C.ident = gs.sb("ident", [128, 128], F32)
    kb.op(kb.DVE, lambda: nc.vector.memset(C.ident.t[:], 1.0), writes=[C.ident])
    kb.op(kb.POOL, lambda: nc.gpsimd.affine_select(out=C.ident.t[:], in_=C.ident.t[:], pattern=[[-1, 128]],
                                                   compare_op=ALU.is_equal, fill=0.0, base=0,
                                                   channel_multiplier=1),
          reads=[C.ident], writes=[C.ident])
    C.wsT = gs.sb("wsT", [128, 8, 128], BF16)
    wsn = gs.sb("wsn", [128, 8, 128], F32)
    kb.dma(kb.SP, wsn.t[:], d["wsp"].rearrange("g i j -> i g j"), writes=[wsn])
    for g in range(8):
        hb = C.psh[3][g // 4]
        kb.op(kb.PE, lambda: nc.tensor.transpose(C.ps[3][:, g * 128:(g + 1) * 128], wsn.t[:, g, :], C.ident.t[:]),
              reads=[wsn, C.ident], writes=[hb])
    kb.op(kb.DVE, lambda: nc.vector.tensor_copy(out=C.wsT.t[:].rearrange("p g i -> p (g i)"), in_=C.ps[3][:]),
          reads=[C.psh[3][0], C.psh[3][1]], writes=[C.wsT])
    kb.op(kb.DVE, lambda: nc.vector.memset(C.wsT.t[64:128, :, 0:64], 0.0), writes=[C.wsT])


def phase1(kb, C, d, t0):
    nc = kb.nc
    PE, ACT, DVE, SP = kb.PE, kb.ACT, kb.DVE, kb.SP
    xTv = d["xT"].rearrange("(c p) t -> p c t", p=128)
    w_in = d["w_in"]
    with kb.scope() as S0:
        hT = [S0.sb(f"hT{c}", [128, TT], BF16) for c in range(16)]
        wring = Ring([S0.sb(f"wr{i}", [128, 16 * 512], BF16) for i in range(NW)])
        with kb.scope() as S1:
            xg = [S1.sb(f"xt{i}", [128, 4, TT], F32) for i in range(4)]
            for i in range(4):
                kb.dma(SP, xg[i].t[:], xTv[:, 4 * i:4 * i + 4, t0:t0 + TT], writes=[xg[i]])
            xt = [(xg[c // 4], xg[c // 4].t[:, c % 4, :]) for c in range(16)]
            rms_to_hT(kb, C, S1, xt, C.gmix, hT, "a")
        with kb.scope() as S2:
            uT = S2.sb("uT", [128, 8, TT], BF16)
            sT = S2.sb("sT", [128, 8, TT], BF16)
            sqq = Ring([S2.sb(f"sqq{i}", [128, TT], BF16) for i in range(2)])
            lq = Ring([S2.sb(f"lq{i}", [128, TT], F32) for i in range(1)])
            rq = Ring([S2.sb(f"rq{i}", [128, TT], F32) for i in range(2)])
            ob = Ring([S2.sb(f"ob{i}", [128, TT], BF16) for i in range(3)])
            vtok = Ring([S2.sb(f"vtok{i}", [128, 1024], BF16) for i in range(2)])
            vln = Ring([S2.sb(f"vln{i}", [128, 1024], BF16) for i in range(2)])
            bst = Ring([S2.sb(f"bst{i}", [128, 6 * 2], F32) for i in range(2)])
            stmp = Ring([S2.sb(f"stmp{i}", [128, 512], F32) for i in range(2)])
            praw = Ring([0, 1])
            pending = []

            def fm_block(slab, kcn, lc, evac):
                pi = praw.next()
                pst, psb = C.ps[pi], C.psh[pi]
                for kc in range(kcn):
                    for h in range(2):
                        last = (kc == kcn - 1 and h == 1)
                        kb.op(PE, lambda: nc.tensor.matmul(
                            pst[:, h * 512:(h + 1) * 512], wslab_view(slab, kcn, 512)[:, kc, lc * 128:(lc + 1) * 128],
                            hT[kc].t[:, h * 512:(h + 1) * 512], start=(kc == 0), stop=(kc == kcn - 1)),
                            reads=[slab, hT[kc]], writes=[psb[h]], inc=last)
                while pending:
                    pending.pop(0)()
                evac(pst, psb)

            def qk_evac(row0, gvec, hd, dst):
                def ev(pst, psb):
                    sq = sqq.next()
                    kb.op(ACT, lambda: nc.scalar.activation(out=sq.t[:], in_=pst[:], func=AF.Square),
                          reads=psb, writes=[sq])

                    def later():
                        for h in range(2):
                            kb.op(PE, lambda: nc.tensor.matmul(C.ps[2][:, h * 512:(h + 1) * 512], C.ones.t[:],
                                                               sq.t[:, h * 512:(h + 1) * 512], start=True, stop=True),
                                  reads=[sq, C.ones], writes=[C.psh[2][h]], inc=(h == 1))
                        l = lq.next()
                        r = rq.next()
                        o = ob.next()
                        kb.op(ACT, lambda: nc.scalar.activation(out=l.t[:], in_=C.ps[2][:], func=AF.Ln,
                                                                scale=1.0 / DH, bias=C.epsb.t[:]),
                              reads=[C.psh[2][0], C.psh[2][1], C.epsb], writes=[l])
                        kb.op(ACT, lambda: nc.scalar.activation(out=r.t[:], in_=l.t[:], func=AF.Exp, scale=-0.5),
                              reads=[l], writes=[r])
                        kb.op(DVE, lambda: nc.vector.scalar_tensor_tensor(
                            out=o.t[:], in0=pst[:], scalar=gvec.t[:, hd:hd + 1], in1=r.t[:],
                            op0=ALU.mult, op1=ALU.mult), reads=[psb[0], psb[1], gvec, r], writes=[o])
                        kb.dma(SP, dst[row0:row0 + 128, t0:t0 + TT], o.t[:], reads=[o])
                    pending.append(later)
                return ev

            def gate_evac(row0):
                def ev(pst, psb):
                    o = ob.next()
                    kb.op(ACT, lambda: nc.scalar.activation(out=o.t[:], in_=pst[:], func=AF.Sigmoid),
                          reads=psb, writes=[o])
                    kb.dma(SP, d["gates"][row0:row0 + 128, t0:t0 + TT], o.t[:], reads=[o])
                return ev

            def u_evac(g):
                def ev(pst, psb):
                    kb.op(ACT, lambda: nc.scalar.activation(out=uT.t[:, g, :], in_=pst[:], func=AF.Gelu),
                          reads=psb, writes=[uT])
                return ev

            for s in range(4):
                slab = load_wslab(kb, wring, w_in, 0, 16, s * 512, 512)
                for lc in range(4):
                    blk = s * 4 + lc
                    if blk < 8:
                        fm_block(slab, 16, lc, qk_evac(blk * 128, C.gqs, blk, d["qT"]))
                    else:
                        fm_block(slab, 16, lc, qk_evac((blk - 8) * 128, C.gk, blk - 8, d["kT"]))
            while pending:
                pending.pop(0)()

            hring = Ring([(0, 0), (0, 1), (1, 0), (1, 1)])

            def tm_group(col0, evac):
                slabs = [load_wslab(kb, wring, w_in, 0, 16, col0 + cs * 512, 512) for cs in range(2)]
                for tb in range(TT // 128):
                    outs = []
                    for cs in range(2):
                        pi, h = hring.next()
                        pst, pb = C.ps[pi], C.psh[pi][h]
                        for kc in range(16):
                            kb.op(PE, lambda: nc.tensor.matmul(
                                pst[:, h * 512:(h + 1) * 512], hT[kc].t[:, tb * 128:(tb + 1) * 128],
                                wslab_view(slabs[cs], 16, 512)[:, kc, :], start=(kc == 0), stop=(kc == 15)),
                                reads=[slabs[cs], hT[kc]], writes=[pb], inc=(kc == 15))
                        outs.append((pst[:, h * 512:(h + 1) * 512], pb))
                    evac(tb, outs)

            def v_evac(tb, outs):
                vt = vtok.next()
                for cs in range(2):
                    ap, pb = outs[cs]
                    kb.op(DVE, lambda: nc.vector.tensor_copy(out=vt.t[:, cs * 512:(cs + 1) * 512], in_=ap),
                          reads=[pb], writes=[vt])
                kb.dma(SP, d["v"][t0 + tb * 128:t0 + (tb + 1) * 128, :], vt.t[:], reads=[vt])

            tm_group(2048, v_evac)
            for s in range(2):
                slab = load_wslab(kb, wring, w_in, 0, 16, 3072 + s * 512, 512)
                for lc in range(4):
                    fm_block(slab, 16, lc, u_evac(s * 4 + lc))

            vgall = [S2.sb(f"vga{i}", [128, 1024], F32) for i in range(TT // 128)]
            mvall = S2.sb("mvall", [128, TT // 128, 2], F32)
            lall = S2.sb("lall", [128, TT // 128], F32)
            rall = S2.sb("rall", [128, TT // 128], F32)

            def vsg_evac(tb, outs):
                g_ = vgall[tb]
                for cs in range(2):
                    ap, pb = outs[cs]
                    kb.op(ACT, lambda: nc.scalar.activation(out=g_.t[:, cs * 512:(cs + 1) * 512], in_=ap, func=AF.Gelu),
                          reads=[pb], writes=[g_])
                st = bst.next()
                for cs in range(2):
                    kb.op(DVE, lambda: nc.vector.bn_stats(out=st.t[:, cs * 6:(cs + 1) * 6],
                                                          in_=g_.t[:, cs * 512:(cs + 1) * 512]),
                          reads=[g_], writes=[st])
                kb.op(DVE, lambda: nc.vector.bn_aggr(out=mvall.t[:, tb, :], in_=st.t[:]), reads=[st], writes=[mvall])

            def sgu_tail(tb):
                g_ = vgall[tb]
                kb.op(DVE, lambda: nc.vector.tensor_scalar(out=g_.t[:], in0=g_.t[:], scalar1=mvall.t[:, tb, 0:1],
                                                           scalar2=rall.t[:, tb:tb + 1], op0=ALU.subtract, op1=ALU.mult),
                      reads=[g_, mvall, rall], writes=[g_])
                kb.op(DVE, lambda: nc.vector.tensor_tensor(out=g_.t[:], in0=g_.t[:], in1=C.lng.t[:], op=ALU.mult),
                      reads=[g_, C.lng], writes=[g_])
                vl = vln.next()
                kb.op(DVE, lambda: nc.vector.tensor_tensor(out=vl.t[:], in0=g_.t[:], in1=C.lnb.t[:], op=ALU.add),
                      reads=[g_, C.lnb], writes=[vl])
                for half in range(2):
                    pb = C.psh[3][half]
                    for gg in range(4):
                        g = half * 4 + gg
                        kb.op(PE, lambda: nc.tensor.matmul(
                            C.ps[3][:, half * 512 + gg * 128: half * 512 + (gg + 1) * 128],
                            vl.t[:, g * 128:(g + 1) * 128], C.wsT.t[:, g, :], start=True, stop=True),
                            reads=[vl, C.wsT], writes=[pb], inc=(gg == 3))
                    tmp = stmp.next()
                    kb.op(DVE, lambda: nc.vector.tensor_tensor(
                        out=tmp.t[:], in0=C.ps[3][:, half * 512:(half + 1) * 512],
                        in1=C.bsbc.t[:, half * 512:(half + 1) * 512], op=ALU.add),
                        reads=[pb, C.bsbc], writes=[tmp])
                    kb.op(DVE, lambda: nc.vector.tensor_tensor(
                        out=sT.t[:, half * 4:(half + 1) * 4, tb * 128:(tb + 1) * 128],
                        in0=tmp.t[:].rearrange("p (g i) -> p g i", g=4),
                        in1=uT.t[:, half * 4:(half + 1) * 4, tb * 128:(tb + 1) * 128], op=ALU.mult),
                        reads=[tmp, uT], writes=[sT])

            tm_group(4096, vsg_evac)
            kb.op(ACT, lambda: nc.scalar.activation(out=lall.t[:], in_=mvall.t[:, :, 1], func=AF.Ln,
                                                    bias=C.epsb.t[:]), reads=[mvall, C.epsb], writes=[lall])
            kb.op(ACT, lambda: nc.scalar.activation(out=rall.t[:], in_=lall.t[:], func=AF.Exp, scale=-0.5),
                  reads=[lall], writes=[rall])
            for s in range(8):
                slab = load_wslab(kb, wring, w_in, 0, 16, 5120 + s * 512, 512)
                for lc in range(4):
                    fm_block(slab, 16, lc, gate_evac((s * 4 + lc) * 128))
                    if lc == 1:
                        sgu_tail(s)
            kb.dma(SP, d["sT"].rearrange("(g p) t -> p g t", p=128)[:, :, t0:t0 + TT], sT.t[:], reads=[sT])


def p2_consts(kb, C):
    nc = kb.nc
    gs = C.gs
    C.negones = gs.sb("negones", [128, 128], BF16)
    kb.op(kb.DVE, lambda: nc.vector.memset(C.negones.t[:], -1.0), writes=[C.negones])
    C.negtri = gs.sb("negtri", [128, 128], BF16)
    C.maskp = [gs.sb(f"maskp{j}", [128, 1024], BF16) for j in range(2)]
    with kb.scope() as ts:
        onesf = ts.sb("onesf", [128, 512], F32)
        kb.op(kb.DVE, lambda: nc.vector.memset(onesf.t[:], 1.0), writes=[onesf])
        mone = ts.sb("monef", [128, 128], F32)
        kb.op(kb.DVE, lambda: nc.vector.memset(mone.t[:], -1.0), writes=[mone])
        trif = ts.sb("trif", [128, 128], F32)
        kb.op(kb.POOL, lambda: nc.gpsimd.affine_select(out=trif.t[:], in_=mone.t[:], pattern=[[-1, 128]],
                                                       compare_op=ALU.is_ge, fill=0.0, base=0, channel_multiplier=1),
              reads=[mone], writes=[trif])
        kb.op(kb.DVE, lambda: nc.vector.tensor_copy(out=C.negtri.t[:], in_=trif.t[:]), reads=[trif], writes=[C.negtri])
        mf = ts.sb("maskf", [128, 512], F32)
        for r in range(4):
            kb.op(kb.POOL, lambda: nc.gpsimd.affine_select(out=mf.t[:], in_=onesf.t[:], pattern=[[1, 512]],
                                                           compare_op=ALU.is_gt, fill=0.0, base=-r * 128,
                                                           channel_multiplier=-1),
                  reads=[onesf], writes=[mf])
            mp = C.maskp[(3 - r) // 2]
            hf = (3 - r) % 2
            kb.op(kb.DVE, lambda: nc.vector.tensor_copy(out=mp.t[:, hf * 512:(hf + 1) * 512], in_=mf.t[:]),
                  reads=[mf], writes=[mp])


def phase2(kb, C, d, S, nheads):
    nc = kb.nc
    PE, ACT, DVE, SP = kb.PE, kb.ACT, kb.DVE, kb.SP
    NKB = S // 128
    NQC = S // 512
    with kb.scope() as S0:
        hb = []
        for i in range(2):
            hb.append(dict(q=S0.sb(f"qh{i}", [128, S], BF16), k=S0.sb(f"kh{i}", [128, S], BF16),
                           v=S0.sb(f"vh{i}", [128, NKB, 128], BF16), o=S0.sb(f"oh{i}", [128, S], BF16)))
        er = Ring([S0.sb(f"e{i}", [128, 1024], F32) for i in range(2)])
        spr = Ring([S0.sb(f"sp{i}", [128, 1024], BF16) for i in range(4)])
        ur = Ring([S0.sb(f"u{i}", [128, 512], BF16) for i in range(2)])
        sar = Ring([S0.sb(f"sa{i}", [128, 512], BF16) for i in range(3)])
        ar = Ring([S0.sb(f"a{i}", [128, 1024], BF16) for i in range(3)])
        pairs = []
        for hd in range(nheads):
            for qc in range(NQC):
                n = 4 * qc + 4
                for j in range(n // 2):
                    pairs.append(dict(hd=hd, qc=qc, j=j, np=n // 2, kA=n - 1 - 2 * j, kB=n - 2 - 2 * j,
                                      diag=j if j < 2 else None))
        state = {}

        def load_head(hd):
            B = hb[hd % 2]
            kb.dma(SP, B["q"].t[:], d["qh"][hd], writes=[B["q"]])
            kb.dma(SP, B["k"].t[:], d["kh"][hd], writes=[B["k"]])
            kb.dma(SP, B["v"].t[:], d["vh"][hd].rearrange("(b p) x -> p b x", p=128), writes=[B["v"]])

        def stageA(idx):
            t = pairs[idx]
            B = hb[t["hd"] % 2]
            if t["qc"] == 0 and t["j"] == 0:
                load_head(t["hd"])
            q0 = t["qc"] * 512
            zi = idx % 2
            zt, zb = C.ps[zi], C.psh[zi]
            for hf, kblk in ((0, t["kA"]), (1, t["kB"])):
                kb.op(PE, lambda: nc.tensor.matmul(zt[:, hf * 512:(hf + 1) * 512], B["k"].t[:, kblk * 128:(kblk + 1) * 128],
                                                   B["q"].t[:, q0:q0 + 512], start=True, stop=True),
                      reads=[B["k"], B["q"]], writes=[zb[hf]], inc=(hf == 1))
            e = er.next()
            kb.op(ACT, lambda: nc.scalar.activation(out=e.t[:], in_=zt[:], func=AF.Exp), reads=zb, writes=[e])
            sp = spr.next()
            kb.op(ACT, lambda: nc.scalar.activation(out=sp.t[:], in_=e.t[:], func=AF.Ln, bias=C.oneb.t[:]),
                  reads=[e, C.oneb], writes=[sp])
            if t["diag"] is not None:
                m = C.maskp[t["diag"]]
                kb.op(DVE, lambda: nc.vector.tensor_tensor(out=sp.t[:], in0=sp.t[:], in1=m.t[:], op=ALU.mult),
                      reads=[sp, m], writes=[sp])
            t["sp"] = sp
            if t["j"] == 0:
                t["sacc"] = None
                ns = sar.next()
                kb.op(DVE, lambda: nc.vector.tensor_tensor(out=ns.t[:], in0=sp.t[:, 0:512], in1=sp.t[:, 512:1024],
                                                           op=ALU.add), reads=[sp], writes=[ns])
                state["next_sacc"] = ns
            else:
                t["sacc"] = state["next_sacc"]
                if t["j"] < t["np"] - 1:
                    u = ur.next()
                    kb.op(DVE, lambda: nc.vector.tensor_tensor(out=u.t[:], in0=sp.t[:, 0:512], in1=sp.t[:, 512:1024],
                                                               op=ALU.add), reads=[sp], writes=[u])
                    ns = sar.next()
                    prev = state["next_sacc"]
                    kb.op(DVE, lambda: nc.vector.tensor_tensor(out=ns.t[:], in0=prev.t[:], in1=u.t[:], op=ALU.add),
                          reads=[prev, u], writes=[ns])
                    state["next_sacc"] = ns

        def stageB(idx):
            t = pairs[idx]
            B = hb[t["hd"] % 2]
            q0 = t["qc"] * 512
            lt, lb = C.ps[3], C.psh[3]
            sp, sacc = t["sp"], t["sacc"]
            has_s = sacc is not None
            kA, kB_ = t["kA"], t["kB"]
            kb.op(PE, lambda: nc.tensor.matmul(lt[:, 0:512], B["k"].t[:, kA * 128:(kA + 1) * 128], B["q"].t[:, q0:q0 + 512],
                                               start=True, stop=False), reads=[B["k"], B["q"]], writes=[lb[0]], inc=False)
            kb.op(PE, lambda: nc.tensor.matmul(lt[:, 0:512], C.negtri.t[:], sp.t[:, 0:512], start=False, stop=not has_s),
                  reads=[C.negtri, sp], writes=[lb[0]], inc=False)
            if has_s:
                kb.op(PE, lambda: nc.tensor.matmul(lt[:, 0:512], C.negones.t[:], sacc.t[:], start=False, stop=True),
                      reads=[C.negones, sacc], writes=[lb[0]], inc=False)
            kb.op(PE, lambda: nc.tensor.matmul(lt[:, 512:1024], B["k"].t[:, kB_ * 128:(kB_ + 1) * 128],
                                               B["q"].t[:, q0:q0 + 512], start=True, stop=False),
                  reads=[B["k"], B["q"]], writes=[lb[1]], inc=False)
            kb.op(PE, lambda: nc.tensor.matmul(lt[:, 512:1024], C.negtri.t[:], sp.t[:, 512:1024], start=False, stop=False),
                  reads=[C.negtri, sp], writes=[lb[1]], inc=False)
            kb.op(PE, lambda: nc.tensor.matmul(lt[:, 512:1024], C.negones.t[:], sp.t[:, 0:512], start=False, stop=not has_s),
                  reads=[C.negones, sp], writes=[lb[1]], inc=not has_s)
            if has_s:
                kb.op(PE, lambda: nc.tensor.matmul(lt[:, 512:1024], C.negones.t[:], sacc.t[:], start=False, stop=True),
                      reads=[C.negones, sacc], writes=[lb[1]], inc=True)
            a = ar.next()
            kb.op(ACT, lambda: nc.scalar.activation(out=a.t[:], in_=lt[:], func=AF.Exp), reads=lb, writes=[a])
            if t["diag"] is not None:
                m = C.maskp[t["diag"]]
                kb.op(DVE, lambda: nc.vector.tensor_tensor(out=a.t[:], in0=a.t[:], in1=m.t[:], op=ALU.mult),
                      reads=[a, m], writes=[a])
            t["a"] = a

        def stageC(idx):
            t = pairs[idx]
            B = hb[t["hd"] % 2]
            q0 = t["qc"] * 512
            oi = t["qc"] % 2
            obuf = C.psh[2][oi]
            oap = C.ps[2][:, oi * 512:(oi + 1) * 512]
            last = (t["j"] == t["np"] - 1)
            a = t["a"]
            kb.op(PE, lambda: nc.tensor.matmul(oap, B["v"].t[:, t["kA"], :], a.t[:, 0:512], start=(t["j"] == 0), stop=False),
                  reads=[B["v"], a], writes=[obuf], inc=False)
            kb.op(PE, lambda: nc.tensor.matmul(oap, B["v"].t[:, t["kB"], :], a.t[:, 512:1024], start=False, stop=last),
                  reads=[B["v"], a], writes=[obuf], inc=True)
            if last:
                kb.op(DVE, lambda: nc.vector.tensor_copy(out=B["o"].t[:, q0:q0 + 512], in_=oap),
                      reads=[obuf], writes=[B["o"]])
                if t["qc"] == NQC - 1:
                    kb.dma(SP, d["oh"][t["hd"]], B["o"].t[:], reads=[B["o"]])

        N = len(pairs)
        for step in range(N + 2):
            if step < N:
                stageA(step)
            if 0 <= step - 1 < N:
                stageB(step - 1)
            if 0 <= step - 2 < N:
                stageC(step - 2)


def p3_consts(kb, C, d, gs=None):
    C.gff = (gs or C.gs).sb("gff", [128, 16], F32)
    kb.dma(kb.SP, C.gff.t[:], d["gff"][:, :], writes=[C.gff])


def phase3(kb, C, d, t0):
    nc = kb.nc
    PE, ACT, DVE, SP = kb.PE, kb.ACT, kb.DVE, kb.SP
    xin = d["xT"].rearrange("(c p) t -> p c t", p=128)
    xout = d["xo"].rearrange("(c p) t -> p c t", p=128)
    pr = Ring([0, 1, 2, 3])
    with kb.scope() as S0:
        xg = [S0.sb(f"x3_{i}", [128, 4, TT], F32) for i in range(4)]
        for i in range(4):
            kb.dma(SP, xg[i].t[:], xin[:, 4 * i:4 * i + 4, t0:t0 + TT], writes=[xg[i]])
        xt = [(xg[c // 4], xg[c // 4].t[:, c % 4, :]) for c in range(16)]
        wring = Ring([S0.sb(f"w3r{i}", [128, 16 * 512], BF16) for i in range(NW)])
        with kb.scope() as S1:
            oT = S1.sb("oT", [128, 8, TT], BF16)
            sT = S1.sb("sT3", [128, 8, TT], BF16)
            kb.dma(SP, oT.t[:], d["oT"].rearrange("(g p) t -> p g t", p=128)[:, :, t0:t0 + TT], writes=[oT])
            kb.dma(SP, sT.t[:], d["sT"].rearrange("(g p) t -> p g t", p=128)[:, :, t0:t0 + TT], writes=[sT])
            mg = [S1.sb(f"mg{c}", [128, TT], BF16) for c in range(16)]
            gar = Ring([S1.sb(f"ga{i}", [128, TT], BF16) for i in range(2)])
            gbr = Ring([S1.sb(f"gb{i}", [128, TT], BF16) for i in range(2)])
            m1r = Ring([S1.sb(f"m1{i}", [128, TT], F32) for i in range(2)])
            m2r = Ring([S1.sb(f"m2{i}", [128, TT], F32) for i in range(2)])
            for half in range(2):
                sa = load_wslab(kb, wring, d["w_oa"], 0, 8, half * 1024, 1024)
                sbb = load_wslab(kb, wring, d["w_ob"], 0, 8, half * 1024, 1024)
                for lc in range(8):
                    f = half * 8 + lc
                    pa, pb_ = pr.next(), pr.next()
                    for (pi, slab, act) in ((pa, sa, oT), (pb_, sbb, sT)):
                        for kc in range(8):
                            for h in range(2):
                                kb.op(PE, lambda: nc.tensor.matmul(
                                    C.ps[pi][:, h * 512:(h + 1) * 512],
                                    wslab_view(slab, 8, 1024)[:, kc, lc * 128:(lc + 1) * 128],
                                    act.t[:, kc, h * 512:(h + 1) * 512], start=(kc == 0), stop=(kc == 7)),
                                    reads=[slab, act], writes=[C.psh[pi][h]], inc=(kc == 7 and h == 1))
                    ga, gb = gar.next(), gbr.next()
                    kb.dma(SP, ga.t[:], d["gates"][f * 128:(f + 1) * 128, t0:t0 + TT], writes=[ga])
                    kb.dma(SP, gb.t[:], d["gates"][D + f * 128:D + (f + 1) * 128, t0:t0 + TT], writes=[gb])
                    m1, m2 = m1r.next(), m2r.next()
                    kb.op(DVE, lambda: nc.vector.tensor_tensor(out=m1.t[:], in0=C.ps[pa][:], in1=ga.t[:], op=ALU.mult),
                          reads=[C.psh[pa][0], C.psh[pa][1], ga], writes=[m1])
                    kb.op(DVE, lambda: nc.vector.tensor_tensor(out=m2.t[:], in0=C.ps[pb_][:], in1=gb.t[:], op=ALU.mult),
                          reads=[C.psh[pb_][0], C.psh[pb_][1], gb], writes=[m2])
                    kb.op(DVE, lambda: nc.vector.tensor_tensor(out=mg[f].t[:], in0=m1.t[:], in1=m2.t[:], op=ALU.add),
                          reads=[m1, m2], writes=[mg[f]])
            for s in range(4):
                slab = load_wslab(kb, wring, d["w_out"], 0, 16, s * 512, 512)
                for lc in range(4):
                    f = s * 4 + lc
                    pi = pr.next()
                    for kc in range(16):
                        for h in range(2):
                            kb.op(PE, lambda: nc.tensor.matmul(
                                C.ps[pi][:, h * 512:(h + 1) * 512],
                                wslab_view(slab, 16, 512)[:, kc, lc * 128:(lc + 1) * 128],
                                mg[kc].t[:, h * 512:(h + 1) * 512], start=(kc == 0), stop=(kc == 15)),
                                reads=[slab, mg[kc]], writes=[C.psh[pi][h]], inc=(kc == 15 and h == 1))
                    xb, xap = xt[f]
                    kb.op(DVE, lambda: nc.vector.tensor_tensor(out=xap, in0=xap, in1=C.ps[pi][:], op=ALU.add),
                          reads=[xb, C.psh[pi][0], C.psh[pi][1]], writes=[xb])
        with kb.scope() as S2:
            hT = [S2.sb(f"h2T{c}", [128, TT], BF16) for c in range(16)]
            rms_to_hT(kb, C, S2, xt, C.gff, hT, "f")
            NG = 16
            hid = [[S2.sb(f"hid{i}_{j}", [128, TT], BF16) for j in range(4)] for i in range(2)]
            rr = Ring([S2.sb(f"rr{i}", [128, TT], F32) for i in range(2)])

            def F1(g):
                slab = load_wslab(kb, wring, d["w_ff1"], 0, 16, g * 512, 512)
                for j in range(4):
                    pi = pr.next()
                    for kc in range(16):
                        for h in range(2):
                            kb.op(PE, lambda: nc.tensor.matmul(
                                C.ps[pi][:, h * 512:(h + 1) * 512],
                                wslab_view(slab, 16, 512)[:, kc, j * 128:(j + 1) * 128],
                                hT[kc].t[:, h * 512:(h + 1) * 512], start=(kc == 0), stop=(kc == 15)),
                                reads=[slab, hT[kc]], writes=[C.psh[pi][h]], inc=(kc == 15 and h == 1))
                    r = rr.next()
                    hb_ = hid[g % 2][j]
                    kb.op(ACT, lambda: nc.scalar.activation(out=r.t[:], in_=C.ps[pi][:], func=AF.Relu),
                          reads=[C.psh[pi][0], C.psh[pi][1]], writes=[r])
                    kb.op(DVE, lambda: nc.vector.tensor_tensor(out=hb_.t[:], in0=r.t[:], in1=r.t[:], op=ALU.mult),
                          reads=[r], writes=[hb_])

            def F2(g):
                slab = load_wslab(kb, wring, d["w_ff2"], g * 512, 4, 0, 2048)
                for o in range(16):
                    pi = pr.next()
                    for j in range(4):
                        for h in range(2):
                            kb.op(PE, lambda: nc.tensor.matmul(
                                C.ps[pi][:, h * 512:(h + 1) * 512],
                                wslab_view(slab, 4, 2048)[:, j, o * 128:(o + 1) * 128],
                                hid[g % 2][j].t[:, h * 512:(h + 1) * 512], start=(j == 0), stop=(j == 3)),
                                reads=[slab, hid[g % 2][j]], writes=[C.psh[pi][h]], inc=(j == 3 and h == 1))
                    xb, xap = xt[o]
                    kb.op(DVE, lambda: nc.vector.tensor_tensor(out=xap, in0=xap, in1=C.ps[pi][:], op=ALU.add),
                          reads=[xb, C.psh[pi][0], C.psh[pi][1]], writes=[xb])

            for g in range(NG + 1):
                if g < NG:
                    F1(g)
                if g >= 1:
                    F2(g - 1)
        for i in range(4):
            kb.dma(SP, xout[:, 4 * i:4 * i + 4, t0:t0 + TT], xg[i].t[:], reads=[xg[i]])


def _in(nc, name, shape, dt):
    return nc.dram_tensor(name, list(shape), dt, kind="ExternalInput").ap()


def _out(nc, name, shape, dt):
    return nc.dram_tensor(name, list(shape), dt, kind="ExternalOutput").ap()


def build_p1(T):
    nc = bass.Bass("TRN2", target_bir_lowering=False)
    d = dict(
        xT=_in(nc, "xT", [D, T], F32), gmix=_in(nc, "gmix", [128, 16], F32), w_in=_in(nc, "w_in", [D, INC], F32),
        gq=_in(nc, "gq", [128, 8], F32), gk=_in(nc, "gk", [128, 8], F32),
        lng=_in(nc, "lng", [1, SGW], F32), lnb=_in(nc, "lnb", [1, SGW], F32),
        wsp=_in(nc, "wsp", [8, 128, 128], F32), bsp=_in(nc, "bsp", [1, 8 * 128], F32),
        qT=_out(nc, "qT", [SBW, T], BF16), kT=_out(nc, "kT", [SBW, T], BF16), v=_out(nc, "v", [T, SBW], BF16),
        sT=_out(nc, "sT", [SGW, T], BF16), gates=_out(nc, "gates", [2 * D, T], BF16),
    )
    with ExitStack() as es:
        kb = KB(nc)
        C = setup_common(kb, es)
        load_consts_eps(kb, C)
        p1_consts(kb, C, d)
        for t0 in range(0, T, TT):
            phase1(kb, C, d, t0)
        kb.finish()
    return nc


def build_p2(S, nheads=2):
    nc = bass.Bass("TRN2", target_bir_lowering=False)
    d = dict(qh=_in(nc, "qh", [nheads, 128, S], BF16), kh=_in(nc, "kh", [nheads, 128, S], BF16),
             vh=_in(nc, "vh", [nheads, S, 128], BF16), oh=_out(nc, "oh", [nheads, 128, S], BF16))
    with ExitStack() as es:
        kb = KB(nc)
        C = setup_common(kb, es)
        load_consts_eps(kb, C)
        p2_consts(kb, C)
        phase2(kb, C, d, S, nheads)
        kb.finish()
    return nc


def build_p3(T):
    nc = bass.Bass("TRN2", target_bir_lowering=False)
    d = dict(
        xT=_in(nc, "xT", [D, T], F32), oT=_in(nc, "oT", [SBW, T], BF16), sT=_in(nc, "sT", [SGW, T], BF16),
        gates=_in(nc, "gates", [2 * D, T], BF16), w_oa=_in(nc, "w_oa", [SBW, D], F32),
        w_ob=_in(nc, "w_ob", [SGW, D], F32), w_out=_in(nc, "w_out", [D, D], F32), gff=_in(nc, "gff", [128, 16], F32),
        w_ff1=_in(nc, "w_ff1", [D, DFF], F32), w_ff2=_in(nc, "w_ff2", [DFF, D], F32),
        xo=_out(nc, "xo", [D, T], F32),
    )
    with ExitStack() as es:
        kb = KB(nc)
        C = setup_common(kb, es)
        load_consts_eps(kb, C)
        p3_consts(kb, C, d)
        for t0 in range(0, T, TT):
            phase3(kb, C, d, t0)
        kb.finish()
    return nc


def build_fused(S, depth):
    nc = bass.Bass("TRN2", target_bir_lowering=False)
    x_in = _in(nc, "xT", [D, S], F32)
    W = dict(
        gmix=_in(nc, "gmix", [depth, 128, 16], F32), w_in=_in(nc, "w_in", [depth, D, INC], F32),
        gq=_in(nc, "gq", [depth, 128, 8], F32), gk=_in(nc, "gk", [depth, 128, 8], F32),
        lng=_in(nc, "lng", [depth, 1, SGW], F32), lnb=_in(nc, "lnb", [depth, 1, SGW], F32),
        wsp=_in(nc, "wsp", [depth, 8, 128, 128], F32), bsp=_in(nc, "bsp", [depth, 1, 8 * 128], F32),
        w_oa=_in(nc, "w_oa", [depth, SBW, D], F32), w_ob=_in(nc, "w_ob", [depth, SGW, D], F32),
        w_out=_in(nc, "w_out", [depth, D, D], F32), gff=_in(nc, "gff", [depth, 128, 16], F32),
        w_ff1=_in(nc, "w_ff1", [depth, D, DFF], F32), w_ff2=_in(nc, "w_ff2", [depth, DFF, D], F32))
    y_out = _out(nc, "yT", [D, S], F32)
    xbuf = nc.dram_tensor("xbuf", [D, S], F32).ap()
    qT = nc.dram_tensor("qT_s", [SBW, S], BF16).ap()
    kT = nc.dram_tensor("kT_s", [SBW, S], BF16).ap()
    v = nc.dram_tensor("v_s", [S, SBW], BF16).ap()
    sT = nc.dram_tensor("sT_s", [SGW, S], BF16).ap()
    gates = nc.dram_tensor("gates_s", [2 * D, S], BF16).ap()
    oT = nc.dram_tensor("oT_s", [SBW, S], BF16).ap()
    with ExitStack() as es:
        kb = KB(nc)
        C = setup_common(kb, es)
        load_consts_eps(kb, C)
        p2_consts(kb, C)
        for l in range(depth):
            src = x_in if l == 0 else xbuf
            dst = y_out if l == depth - 1 else xbuf
            d = {k: a[l] for k, a in W.items()}
            d.update(xT=src, xo=dst, qT=qT, kT=kT, v=v, sT=sT, gates=gates, oT=oT)
            d2 = dict(qh=qT.rearrange("(h p) s -> h p s", p=128), kh=kT.rearrange("(h p) s -> h p s", p=128),
                      vh=v.rearrange("s (h x) -> h s x", h=NH), oh=oT.rearrange("(h p) s -> h p s", p=128))
            with kb.scope() as LS:
                p1_consts(kb, C, d, LS)
                for t0 in range(0, S, TT):
                    phase1(kb, C, d, t0)
            kb.barrier()
            phase2(kb, C, d2, S, NH)
            kb.barrier()
            with kb.scope() as LS:
                p3_consts(kb, C, d, LS)
                for t0 in range(0, S, TT):
                    phase3(kb, C, d, t0)
            kb.barrier()
        kb.finish()
    return nc


def kernel_fused(x, g_mix, w_in, g_q, g_k, sgu_ln_g, sgu_ln_b, w_spatial, b_spatial, w_oa, w_ob, w_out, g_ff,
                 w_ff1, w_ff2):
    x = np.asarray(x)
    B, S, _ = x.shape
    f32 = np.float32
    depth = np.asarray(w_in).shape[0]
    gpb = NCORES // B
    cores = list(range(NCORES))
    shared = dict(
        gmix=_c(np.asarray(g_mix, f32).reshape(depth, 16, 128).transpose(0, 2, 1)), w_in=_c(np.asarray(w_in, f32)),
        gq=_c(np.asarray(g_q, f32).transpose(0, 2, 1)), gk=_c(np.asarray(g_k, f32).transpose(0, 2, 1)),
        lng=_c(np.asarray(sgu_ln_g, f32).reshape(depth, 1, SGW)), lnb=_c(np.asarray(sgu_ln_b, f32).reshape(depth, 1, SGW)),
        wsp=_c(np.asarray(w_spatial, f32)), bsp=_c(np.asarray(b_spatial, f32).reshape(depth, 1, 8 * 128)),
        w_oa=_c(np.asarray(w_oa, f32)), w_ob=_c(np.asarray(w_ob, f32)), w_out=_c(np.asarray(w_out, f32)),
        gff=_c(np.asarray(g_ff, f32).reshape(depth, 16, 128).transpose(0, 2, 1)),
        w_ff1=_c(np.asarray(w_ff1, f32)), w_ff2=_c(np.asarray(w_ff2, f32)))
    xTs = [_c(x[b].T) for b in range(B)]
    nc = _prog(("fused", S, depth), lambda: build_fused(S, depth))
    res = run_bass_kernel_spmd(nc, [dict(shared, xT=xTs[c // gpb]) for c in cores], core_ids=cores).results
    out = np.empty((B, S, D), dtype=np.float32)
    for b in range(B):
        out[b] = res[b * gpb]["yT"].T
    return out


_PROGS = {}


def _prog(key, fn):
    if key not in _PROGS:
        _PROGS[key] = fn()
    return _PROGS[key]


def _c(a):
    return np.ascontiguousarray(a)


def kernel_unfused(x, g_mix, w_in, g_q, g_k, sgu_ln_g, sgu_ln_b, w_spatial, b_spatial, w_oa, w_ob, w_out, g_ff, w_ff1,
           w_ff2):
    x = np.asarray(x)
    B, S, _ = x.shape
    depth = np.asarray(w_in).shape[0]
    gpb = NCORES // B
    T = S // gpb
    hpc = NH // gpb
    cores = list(range(NCORES))
    xT = [_c(x[c // gpb, (c % gpb) * T:(c % gpb + 1) * T, :].T) for c in cores]
    f32 = np.float32
    for l in range(depth):
        wl = _c(np.asarray(w_in[l], dtype=f32))
        common1 = dict(
            gmix=_c(np.asarray(g_mix[l], f32).reshape(16, 128).T), w_in=wl,
            gq=_c(np.asarray(g_q[l], f32).T), gk=_c(np.asarray(g_k[l], f32).T),
            lng=_c(np.asarray(sgu_ln_g[l], f32).reshape(1, SGW)), lnb=_c(np.asarray(sgu_ln_b[l], f32).reshape(1, SGW)),
            wsp=_c(np.asarray(w_spatial[l], f32)), bsp=_c(np.asarray(b_spatial[l], f32).reshape(1, 8 * 128)))
        r1 = run_bass_kernel_spmd(_prog(("p1", T), lambda: build_p1(T)),
                                  [dict(common1, xT=xT[c]) for c in cores], core_ids=cores).results
        in2 = []
        for c in cores:
            b, hp = c // gpb, c % gpb
            src = [r1[b * gpb + j] for j in range(gpb)]
            rows = slice(hp * hpc * 128, (hp + 1) * hpc * 128)
            qh = np.concatenate([s_["qT"][rows] for s_ in src], axis=1).reshape(hpc, 128, S)
            kh = np.concatenate([s_["kT"][rows] for s_ in src], axis=1).reshape(hpc, 128, S)
            vh = np.concatenate([s_["v"][:, rows] for s_ in src], axis=0).reshape(S, hpc, 128).transpose(1, 0, 2)
            in2.append(dict(qh=_c(qh), kh=_c(kh), vh=_c(vh)))
        r2 = run_bass_kernel_spmd(_prog(("p2", S, hpc), lambda: build_p2(S, hpc)), in2, core_ids=cores).results
        common3 = dict(w_oa=_c(np.asarray(w_oa[l], f32)), w_ob=_c(np.asarray(w_ob[l], f32)),
                       w_out=_c(np.asarray(w_out[l], f32)), gff=_c(np.asarray(g_ff[l], f32).reshape(16, 128).T),
                       w_ff1=_c(np.asarray(w_ff1[l], f32)), w_ff2=_c(np.asarray(w_ff2[l], f32)))
        in3 = []
        for c in cores:
            b, j = c // gpb, c % gpb
            oT = np.concatenate([r2[b * gpb + hp]["oh"][:, :, j * T:(j + 1) * T].reshape(hpc * 128, T)
                                 for hp in range(gpb)], axis=0)
            in3.append(dict(common3, xT=xT[c], oT=_c(oT), sT=r1[c]["sT"], gates=r1[c]["gates"]))
        r3 = run_bass_kernel_spmd(_prog(("p3", T), lambda: build_p3(T)), in3, core_ids=cores).results
        xT = [r3[c]["xo"] for c in cores]
    out = np.empty((B, S, D), dtype=np.float32)
    for c in cores:
        out[c // gpb, (c % gpb) * T:(c % gpb + 1) * T, :] = xT[c].T
    return out


FUSED = False


def kernel(**inputs):
    return kernel_fused(**inputs) if FUSED else kernel_unfused(**inputs)
```

```python
import numpy as np
from contextlib import ExitStack, contextmanager
import ml_dtypes

import concourse.bass as bass
import concourse.mybir as mybir
from concourse.bass_utils import run_bass_kernel_spmd

F32 = mybir.dt.float32
BF16 = mybir.dt.bfloat16
AF = mybir.ActivationFunctionType
ALU = mybir.AluOpType

D = 2048
NH = 8
DH = 128
SBW = 1024
SGW = 1024
DFF = 8192
INC = 9216
EPS = 1e-6
TT = 1024
NCORES = 8
NW = 3


class Sem:
    def __init__(self, nc, name):
        self.h = nc.alloc_semaphore(name=name)
        self.count = 0


class Eng:
    def __init__(self, nc, name, e):
        self.e = e
        self.name = name
        self.sem = Sem(nc, "prog_" + name)
        self.seen = {}

    def wait(self, sem, val):
        if val <= 0 or self.seen.get(sem, 0) >= val:
            return
        self.e.wait_ge(sem.h, val)
        self.seen[sem] = val


class Buf:
    def __init__(self, t, fence):
        self.t = t
        self.w = None
        self.r = dict(fence)


class Scope:
    def __init__(self, kb):
        self.kb = kb
        self.es = ExitStack()
        self.bufs = []

    def sb(self, name, shape, dtype):
        kb = self.kb
        kb.uid += 1
        t = self.es.enter_context(kb.nc.sbuf_tensor(f"{name}_{kb.uid}", list(shape), dtype))
        b = Buf(t, kb.fence)
        self.bufs.append(b)
        return b

    def close(self):
        kb = self.kb
        for b in self.bufs:
            if b.w is not None:
                kb.fence[b.w[0]] = max(kb.fence.get(b.w[0], 0), b.w[1])
            for s, v in b.r.items():
                kb.fence[s] = max(kb.fence.get(s, 0), v)
        self.es.close()


class KB:
    def __init__(self, nc):
        self.nc = nc
        self.uid = 0
        self.fence = {}
        self.PE = Eng(nc, "pe", nc.tensor)
        self.ACT = Eng(nc, "act", nc.scalar)
        self.DVE = Eng(nc, "dve", nc.vector)
        self.POOL = Eng(nc, "pool", nc.gpsimd)
        self.SP = Eng(nc, "sp", nc.sync)
        self.engs = [self.PE, self.ACT, self.DVE, self.POOL, self.SP]
        self.dpool = {}
        self.dpi = {}
        for q in (self.SP, self.POOL, self.ACT):
            self.dpool[q] = [Sem(nc, f"dma_{q.name}_{i}") for i in range(8)]
            self.dpi[q] = 0
        self.gbufs = []

    @contextmanager
    def scope(self):
        s = Scope(self)
        try:
            yield s
        finally:
            s.close()

    def _deps(self, eng, reads, writes):
        for b in reads:
            if b.w is not None:
                self._w(eng, b.w)
        for b in writes:
            if b.w is not None:
                self._w(eng, b.w)
            for s, v in b.r.items():
                self._w(eng, (s, v))

    def _w(self, eng, st):
        sem, val = st
        if sem is eng.sem and eng is self.PE:
            return
        eng.wait(sem, val)

    def _stamp(self, st, reads, writes):
        for b in reads:
            b.r[st[0]] = max(b.r.get(st[0], 0), st[1])
        for b in writes:
            b.w = st
            b.r = {}

    def op(self, eng, fn, reads=(), writes=(), inc=True):
        self._deps(eng, reads, writes)
        ins = fn()
        if inc:
            eng.sem.count += 1
            ins.then_inc(eng.sem.h, 1)
            st = (eng.sem, eng.sem.count)
        else:
            st = (eng.sem, eng.sem.count + 1)
        self._stamp(st, reads, writes)

    def dma(self, q, out_ap, in_ap, reads=(), writes=()):
        self._deps(q, reads, writes)
        pool = self.dpool[q]
        sem = pool[self.dpi[q] % len(pool)]
        self.dpi[q] += 1
        q.wait(sem, sem.count)
        ins = q.e.dma_start(out=out_ap, in_=in_ap)
        sem.count += 16
        ins.then_inc(sem.h, 16)
        self._stamp((sem, sem.count), reads, writes)

    def barrier(self):
        sems = [e.sem for e in self.engs]
        for q in self.dpool:
            sems += self.dpool[q]
        for e in self.engs:
            for s in sems:
                if s is e.sem:
                    continue
                e.wait(s, s.count)

    def finish(self):
        for q in self.dpool:
            for s in self.dpool[q]:
                self.SP.wait(s, s.count)
        for e in self.engs:
            if e is not self.SP:
                self.SP.wait(e.sem, e.sem.count)


class Ring:
    def __init__(self, bufs):
        self.bufs = bufs
        self.i = 0

    def next(self):
        b = self.bufs[self.i % len(self.bufs)]
        self.i += 1
        return b


class Ctx:
    pass


def setup_common(kb, es):
    nc = kb.nc
    C = Ctx()
    C.ps = []
    C.psh = []
    for i in range(4):
        t = es.enter_context(nc.psum_tensor(f"ps{i}", [128, 1024], F32))
        C.ps.append(t)
        C.psh.append([Buf(t, {}), Buf(t, {})])
    gs = Scope(kb)
    es.callback(gs.close)
    C.gs = gs
    C.ones = gs.sb("ones", [128, 128], BF16)
    kb.op(kb.DVE, lambda: nc.vector.memset(C.ones.t[:], 1.0), writes=[C.ones])
    return C


def wslab_view(buf, kcn, ncols):
    return buf.t[:, 0:kcn * ncols].rearrange("p (k n) -> p k n", k=kcn)


def load_wslab(kb, ring, w2d, r0, kcn, c0, ncols):
    buf = ring.next()
    src = w2d[r0:r0 + kcn * 128, c0:c0 + ncols].rearrange("(k p) n -> p k n", p=128)
    kb.dma(kb.POOL, wslab_view(buf, kcn, ncols), src, writes=[buf])
    return buf


def rms_to_hT(kb, C, S, xt, gvec, hT, tag):
    nc = kb.nc
    sqr = Ring([S.sb(f"sq{tag}{i}", [128, TT], BF16) for i in range(2)])
    lnv = S.sb(f"lnv{tag}", [128, TT], F32)
    rstd = S.sb(f"rstd{tag}", [128, TT], F32)
    ssb = C.psh[2]
    sst = C.ps[2]
    for c in range(16):
        sq = sqr.next()
        xb, xap = xt[c]
        kb.op(kb.ACT, lambda: nc.scalar.activation(out=sq.t[:], in_=xap, func=AF.Square),
              reads=[xb], writes=[sq])
        for h in range(2):
            kb.op(kb.PE, lambda: nc.tensor.matmul(sst[:, h * 512:(h + 1) * 512], C.ones.t[:],
                                                  sq.t[:, h * 512:(h + 1) * 512],
                                                  start=(c == 0), stop=(c == 15)),
                  reads=[sq, C.ones], writes=[ssb[h]], inc=(h == 1))
    kb.op(kb.ACT, lambda: nc.scalar.activation(out=lnv.t[:], in_=sst[:], func=AF.Ln,
                                               scale=1.0 / D, bias=C.epsb.t[:]),
          reads=[ssb[0], ssb[1], C.epsb], writes=[lnv])
    kb.op(kb.ACT, lambda: nc.scalar.activation(out=rstd.t[:], in_=lnv.t[:], func=AF.Exp, scale=-0.5),
          reads=[lnv], writes=[rstd])
    for c in range(16):
        xb, xap = xt[c]
        kb.op(kb.DVE, lambda: nc.vector.scalar_tensor_tensor(
            out=hT[c].t[:], in0=xap, scalar=gvec.t[:, c:c + 1], in1=rstd.t[:],
            op0=ALU.mult, op1=ALU.mult), reads=[xb, gvec, rstd], writes=[hT[c]])


def load_consts_eps(kb, C):
    nc = kb.nc
    C.epsb = C.gs.sb("epsb", [128, 1], F32)
    kb.op(kb.DVE, lambda: nc.vector.memset(C.epsb.t[:], EPS), writes=[C.epsb])
    C.oneb = C.gs.sb("oneb", [128, 1], F32)
    kb.op(kb.DVE, lambda: nc.vector.memset(C.oneb.t[:], 1.0), writes=[C.oneb])


def p1_consts(kb, C, d, gs=None):
    nc = kb.nc
    gs = gs or C.gs
    C.gmix = gs.sb("gmix", [128, 16], F32)
    kb.dma(kb.SP, C.gmix.t[:], d["gmix"][:, :], writes=[C.gmix])
    C.gq = gs.sb("gq", [128, 8], F32)
    kb.dma(kb.SP, C.gq.t[:], d["gq"][:, :], writes=[C.gq])
    C.gk = gs.sb("gk", [128, 8], F32)
    kb.dma(kb.SP, C.gk.t[:], d["gk"][:, :], writes=[C.gk])
    C.gqs = gs.sb("gqs", [128, 8], F32)
    kb.op(kb.DVE, lambda: nc.vector.tensor_scalar(out=C.gqs.t[:], in0=C.gq.t[:], scalar1=float(DH ** -0.5),
                                                  scalar2=None, op0=ALU.mult),
          reads=[C.gq], writes=[C.gqs])
    C.lng = gs.sb("lng", [128, SGW], F32)
    kb.dma(kb.SP, C.lng.t[:], d["lng"][0:1, :].partition_broadcast(128), writes=[C.lng])
    C.lnb = gs.sb("lnb", [128, SGW], F32)
    kb.dma(kb.SP, C.lnb.t[:], d["lnb"][0:1, :].partition_broadcast(128), writes=[C.lnb])
    C.bsbc = gs.sb("bsbc", [128, 8 * 128], F32)
    kb.dma(kb.SP, C.bsbc.t[:], d["bsp"][0:1, :].partition_broadcast(128), writes=[C.bsbc])
    C.ident = gs.sb("ident", [128, 128], F32)
    kb.op(kb.DVE, lambda: nc.vector.memset(C.ident.t[:], 1.0), writes=[C.ident])
    kb.op(kb.POOL, lambda: nc.gpsimd.affine_select(out=C.ident.t[:], in_=C.ident.t[:], pattern=[[-1, 128]],
                                                   compare_op=ALU.is_equal, fill=0.0, base=0,
                                                   channel_multiplier=1),
          reads=[C.ident], writes=[C.ident])
    C.wsT = gs.sb("wsT", [128, 8, 128], BF16)
    wsn = gs.sb("wsn", [128, 8, 128], F32)
    kb.dma(kb.SP, wsn.t[:], d["wsp"].rearrange("g i j -> i g j"), writes=[wsn])
    for g in range(8):
        hb = C.psh[3][g // 4]
        kb.op(kb.PE, lambda: nc.tensor.transpose(C.ps[3][:, g * 128:(g + 1) * 128], wsn.t[:, g, :], C.ident.t[:]),
              reads=[wsn, C.ident], writes=[hb])
    kb.op(kb.DVE, lambda: nc.vector.tensor_copy(out=C.wsT.t[:].rearrange("p g i -> p (g i)"), in_=C.ps[3][:]),
          reads=[C.psh[3][0], C.psh[3][1]], writes=[C.wsT])
    kb.op(kb.DVE, lambda: nc.vector.memset(C.wsT.t[64:128, :, 0:64], 0.0), writes=[C.wsT])


def phase1(kb, C, d, t0):
    nc = kb.nc
    PE, ACT, DVE, SP = kb.PE, kb.ACT, kb.DVE, kb.SP
    xTv = d["xT"].rearrange("(c p) t -> p c t", p=128)
    w_in = d["w_in"]
    with kb.scope() as S0:
        hT = [S0.sb(f"hT{c}", [128, TT], BF16) for c in range(16)]
        wring = Ring([S0.sb(f"wr{i}", [128, 16 * 512], BF16) for i in range(NW)])
        with kb.scope() as S1:
            xg = [S1.sb(f"xt{i}", [128, 4, TT], F32) for i in range(4)]
            for i in range(4):
                kb.dma(SP, xg[i].t[:], xTv[:, 4 * i:4 * i + 4, t0:t0 + TT], writes=[xg[i]])
            xt = [(xg[c // 4], xg[c // 4].t[:, c % 4, :]) for c in range(16)]
            rms_to_hT(kb, C, S1, xt, C.gmix, hT, "a")
        with kb.scope() as S2:
            uT = S2.sb("uT", [128, 8, TT], BF16)
            sT = S2.sb("sT", [128, 8, TT], BF16)
            sqq = Ring([S2.sb(f"sqq{i}", [128, TT], BF16) for i in range(2)])
            lq = Ring([S2.sb(f"lq{i}", [128, TT], F32) for i in range(1)])
            rq = Ring([S2.sb(f"rq{i}", [128, TT], F32) for i in range(2)])
            ob = Ring([S2.sb(f"ob{i}", [128, TT], BF16) for i in range(3)])
            vtok = Ring([S2.sb(f"vtok{i}", [128, 1024], BF16) for i in range(2)])
            vln = Ring([S2.sb(f"vln{i}", [128, 1024], BF16) for i in range(2)])
            bst = Ring([S2.sb(f"bst{i}", [128, 6 * 2], F32) for i in range(2)])
            stmp = Ring([S2.sb(f"stmp{i}", [128, 512], F32) for i in range(2)])
            praw = Ring([0, 1])
            pending = []

            praw3 = Ring([0, 1, 3])

            def fm_block(slab, kcn, lc, evac, ring=None):
                pi = (ring or praw).next()
                pst, psb = C.ps[pi], C.psh[pi]
                for kc in range(kcn):
                    for h in range(2):
                        last = (kc == kcn - 1 and h == 1)
                        kb.op(PE, lambda: nc.tensor.matmul(
                            pst[:, h * 512:(h + 1) * 512], wslab_view(slab, kcn, 512)[:, kc, lc * 128:(lc + 1) * 128],
                            hT[kc].t[:, h * 512:(h + 1) * 512], start=(kc == 0), stop=(kc == kcn - 1)),
                            reads=[slab, hT[kc]], writes=[psb[h]], inc=last)
                while pending:
                    pending.pop(0)()
                evac(pst, psb)

            def qk_evac(row0, gvec, hd, dst):
                def ev(pst, psb):
                    sq = sqq.next()
                    kb.op(ACT, lambda: nc.scalar.activation(out=sq.t[:], in_=pst[:], func=AF.Square),
                          reads=psb, writes=[sq])

                    def later():
                        for h in range(2):
                            kb.op(PE, lambda: nc.tensor.matmul(C.ps[2][:, h * 512:(h + 1) * 512], C.ones.t[:],
                                                               sq.t[:, h * 512:(h + 1) * 512], start=True, stop=True),
                                  reads=[sq, C.ones], writes=[C.psh[2][h]], inc=(h == 1))
                        l = lq.next()
                        r = rq.next()
                        o = ob.next()
                        kb.op(ACT, lambda: nc.scalar.activation(out=l.t[:], in_=C.ps[2][:], func=AF.Ln,
                                                                scale=1.0 / DH, bias=C.epsb.t[:]),
                              reads=[C.psh[2][0], C.psh[2][1], C.epsb], writes=[l])
                        kb.op(ACT, lambda: nc.scalar.activation(out=r.t[:], in_=l.t[:], func=AF.Exp, scale=-0.5),
                              reads=[l], writes=[r])
                        kb.op(DVE, lambda: nc.vector.scalar_tensor_tensor(
                            out=o.t[:], in0=pst[:], scalar=gvec.t[:, hd:hd + 1], in1=r.t[:],
                            op0=ALU.mult, op1=ALU.mult), reads=[psb[0], psb[1], gvec, r], writes=[o])
                        kb.dma(SP, dst[row0:row0 + 128, t0:t0 + TT], o.t[:], reads=[o])
                    pending.append(later)
                return ev

            def gate_evac(row0):
                def ev(pst, psb):
                    o = ob.next()
                    kb.op(ACT, lambda: nc.scalar.activation(out=o.t[:], in_=pst[:], func=AF.Sigmoid),
                          reads=psb, writes=[o])
                    kb.dma(SP, d["gates"][row0:row0 + 128, t0:t0 + TT], o.t[:], reads=[o])
                return ev

            def u_evac(g):
                def ev(pst, psb):
                    kb.op(ACT, lambda: nc.scalar.activation(out=uT.t[:, g, :], in_=pst[:], func=AF.Gelu),
                          reads=psb, writes=[uT])
                return ev

            for s in range(4):
                slab = load_wslab(kb, wring, w_in, 0, 16, s * 512, 512)
                for lc in range(4):
                    blk = s * 4 + lc
                    if blk < 8:
                        fm_block(slab, 16, lc, qk_evac(blk * 128, C.gqs, blk, d["qT"]), praw3)
                    else:
                        fm_block(slab, 16, lc, qk_evac((blk - 8) * 128, C.gk, blk - 8, d["kT"]), praw3)
            while pending:
                pending.pop(0)()

            hring = Ring([(0, 0), (0, 1), (1, 0), (1, 1)])

            def tm_group(col0, evac):
                slabs = [load_wslab(kb, wring, w_in, 0, 16, col0 + cs * 512, 512) for cs in range(2)]
                for tb in range(TT // 128):
                    outs = []
                    for cs in range(2):
                        pi, h = hring.next()
                        pst, pb = C.ps[pi], C.psh[pi][h]
                        for kc in range(16):
                            kb.op(PE, lambda: nc.tensor.matmul(
                                pst[:, h * 512:(h + 1) * 512], hT[kc].t[:, tb * 128:(tb + 1) * 128],
                                wslab_view(slabs[cs], 16, 512)[:, kc, :], start=(kc == 0), stop=(kc == 15)),
                                reads=[slabs[cs], hT[kc]], writes=[pb], inc=(kc == 15))
                        outs.append((pst[:, h * 512:(h + 1) * 512], pb))
                    evac(tb, outs)

            def v_evac(tb, outs):
                vt = vtok.next()
                for cs in range(2):
                    ap, pb = outs[cs]
                    kb.op(DVE, lambda: nc.vector.tensor_copy(out=vt.t[:, cs * 512:(cs + 1) * 512], in_=ap),
                          reads=[pb], writes=[vt])
                kb.dma(SP, d["v"][t0 + tb * 128:t0 + (tb + 1) * 128, :], vt.t[:], reads=[vt])

            tm_group(2048, v_evac)
            for s in range(2):
                slab = load_wslab(kb, wring, w_in, 0, 16, 3072 + s * 512, 512)
                for lc in range(4):
                    fm_block(slab, 16, lc, u_evac(s * 4 + lc))

            vgall = [S2.sb(f"vga{i}", [128, 1024], F32) for i in range(TT // 128)]
            mvall = S2.sb("mvall", [128, TT // 128, 2], F32)
            lall = S2.sb("lall", [128, TT // 128], F32)
            rall = S2.sb("rall", [128, TT // 128], F32)

            def vsg_evac(tb, outs):
                g_ = vgall[tb]
                for cs in range(2):
                    ap, pb = outs[cs]
                    kb.op(ACT, lambda: nc.scalar.activation(out=g_.t[:, cs * 512:(cs + 1) * 512], in_=ap, func=AF.Gelu),
                          reads=[pb], writes=[g_])
                st = bst.next()
                for cs in range(2):
                    kb.op(DVE, lambda: nc.vector.bn_stats(out=st.t[:, cs * 6:(cs + 1) * 6],
                                                          in_=g_.t[:, cs * 512:(cs + 1) * 512]),
                          reads=[g_], writes=[st])
                kb.op(DVE, lambda: nc.vector.bn_aggr(out=mvall.t[:, tb, :], in_=st.t[:]), reads=[st], writes=[mvall])

            def sgu_tail(tb):
                g_ = vgall[tb]
                kb.op(DVE, lambda: nc.vector.tensor_scalar(out=g_.t[:], in0=g_.t[:], scalar1=mvall.t[:, tb, 0:1],
                                                           scalar2=rall.t[:, tb:tb + 1], op0=ALU.subtract, op1=ALU.mult),
                      reads=[g_, mvall, rall], writes=[g_])
                kb.op(DVE, lambda: nc.vector.tensor_tensor(out=g_.t[:], in0=g_.t[:], in1=C.lng.t[:], op=ALU.mult),
                      reads=[g_, C.lng], writes=[g_])
                vl = vln.next()
                kb.op(DVE, lambda: nc.vector.tensor_tensor(out=vl.t[:], in0=g_.t[:], in1=C.lnb.t[:], op=ALU.add),
                      reads=[g_, C.lnb], writes=[vl])
                for half in range(2):
                    pb = C.psh[3][half]
                    for gg in range(4):
                        g = half * 4 + gg
                        kb.op(PE, lambda: nc.tensor.matmul(
                            C.ps[3][:, half * 512 + gg * 128: half * 512 + (gg + 1) * 128],
                            vl.t[:, g * 128:(g + 1) * 128], C.wsT.t[:, g, :], start=True, stop=True),
                            reads=[vl, C.wsT], writes=[pb], inc=(gg == 3))
                    tmp = stmp.next()
                    kb.op(DVE, lambda: nc.vector.tensor_tensor(
                        out=tmp.t[:], in0=C.ps[3][:, half * 512:(half + 1) * 512],
                        in1=C.bsbc.t[:, half * 512:(half + 1) * 512], op=ALU.add),
                        reads=[pb, C.bsbc], writes=[tmp])
                    kb.op(DVE, lambda: nc.vector.tensor_tensor(
                        out=sT.t[:, half * 4:(half + 1) * 4, tb * 128:(tb + 1) * 128],
                        in0=tmp.t[:].rearrange("p (g i) -> p g i", g=4),
                        in1=uT.t[:, half * 4:(half + 1) * 4, tb * 128:(tb + 1) * 128], op=ALU.mult),
                        reads=[tmp, uT], writes=[sT])

            tm_group(4096, vsg_evac)
            kb.op(ACT, lambda: nc.scalar.activation(out=lall.t[:], in_=mvall.t[:, :, 1], func=AF.Ln,
                                                    bias=C.epsb.t[:]), reads=[mvall, C.epsb], writes=[lall])
            kb.op(ACT, lambda: nc.scalar.activation(out=rall.t[:], in_=lall.t[:], func=AF.Exp, scale=-0.5),
                  reads=[lall], writes=[rall])
            for s in range(8):
                slab = load_wslab(kb, wring, w_in, 0, 16, 5120 + s * 512, 512)
                for lc in range(4):
                    fm_block(slab, 16, lc, gate_evac((s * 4 + lc) * 128))
                    if lc == 1:
                        sgu_tail(s)
            kb.dma(SP, d["sT"].rearrange("(g p) t -> p g t", p=128)[:, :, t0:t0 + TT], sT.t[:], reads=[sT])


def p2_consts(kb, C):
    nc = kb.nc
    gs = C.gs
    C.negones = gs.sb("negones", [128, 128], BF16)
    kb.op(kb.DVE, lambda: nc.vector.memset(C.negones.t[:], -1.0), writes=[C.negones])
    C.negtri = gs.sb("negtri", [128, 128], BF16)
    C.maskp = [gs.sb(f"maskp{j}", [128, 1024], BF16) for j in range(2)]
    with kb.scope() as ts:
        onesf = ts.sb("onesf", [128, 512], F32)
        kb.op(kb.DVE, lambda: nc.vector.memset(onesf.t[:], 1.0), writes=[onesf])
        mone = ts.sb("monef", [128, 128], F32)
        kb.op(kb.DVE, lambda: nc.vector.memset(mone.t[:], -1.0), writes=[mone])
        trif = ts.sb("trif", [128, 128], F32)
        kb.op(kb.POOL, lambda: nc.gpsimd.affine_select(out=trif.t[:], in_=mone.t[:], pattern=[[-1, 128]],
                                                       compare_op=ALU.is_ge, fill=0.0, base=0, channel_multiplier=1),
              reads=[mone], writes=[trif])
        kb.op(kb.DVE, lambda: nc.vector.tensor_copy(out=C.negtri.t[:], in_=trif.t[:]), reads=[trif], writes=[C.negtri])
        mf = ts.sb("maskf", [128, 512], F32)
        for r in range(4):
            kb.op(kb.POOL, lambda: nc.gpsimd.affine_select(out=mf.t[:], in_=onesf.t[:], pattern=[[1, 512]],
                                                           compare_op=ALU.is_gt, fill=0.0, base=-r * 128,
                                                           channel_multiplier=-1),
                  reads=[onesf], writes=[mf])
            mp = C.maskp[(3 - r) // 2]
            hf = (3 - r) % 2
            kb.op(kb.DVE, lambda: nc.vector.tensor_copy(out=mp.t[:, hf * 512:(hf + 1) * 512], in_=mf.t[:]),
                  reads=[mf], writes=[mp])


def phase2(kb, C, d, S, nheads):
    nc = kb.nc
    PE, ACT, DVE, SP = kb.PE, kb.ACT, kb.DVE, kb.SP
    NKB = S // 128
    NQC = S // 512
    with kb.scope() as S0:
        hb = []
        for i in range(2):
            hb.append(dict(q=S0.sb(f"qh{i}", [128, S], BF16), k=S0.sb(f"kh{i}", [128, S], BF16),
                           v=S0.sb(f"vh{i}", [128, NKB, 128], BF16), o=S0.sb(f"oh{i}", [128, S], BF16)))
        er = Ring([S0.sb(f"e{i}", [128, 1024], F32) for i in range(2)])
        spr = Ring([S0.sb(f"sp{i}", [128, 1024], BF16) for i in range(4)])
        ur = Ring([S0.sb(f"u{i}", [128, 512], BF16) for i in range(2)])
        sar = Ring([S0.sb(f"sa{i}", [128, 512], BF16) for i in range(3)])
        ar = Ring([S0.sb(f"a{i}", [128, 1024], BF16) for i in range(3)])
        pairs = []
        for hd in range(nheads):
            for qc in range(NQC):
                n = 4 * qc + 4
                for j in range(n // 2):
                    pairs.append(dict(hd=hd, qc=qc, j=j, np=n // 2, kA=n - 1 - 2 * j, kB=n - 2 - 2 * j,
                                      diag=j if j < 2 else None))
        state = {}

        def load_head(hd):
            B = hb[hd % 2]
            kb.dma(SP, B["q"].t[:], d["qh"][hd], writes=[B["q"]])
            kb.dma(SP, B["k"].t[:], d["kh"][hd], writes=[B["k"]])
            kb.dma(SP, B["v"].t[:], d["vh"][hd].rearrange("(b p) x -> p b x", p=128), writes=[B["v"]])

        def stageA(idx):
            t = pairs[idx]
            B = hb[t["hd"] % 2]
            if t["qc"] == 0 and t["j"] == 0:
                load_head(t["hd"])
            q0 = t["qc"] * 512
            zi = idx % 2
            zt, zb = C.ps[zi], C.psh[zi]
            for hf, kblk in ((0, t["kA"]), (1, t["kB"])):
                kb.op(PE, lambda: nc.tensor.matmul(zt[:, hf * 512:(hf + 1) * 512], B["k"].t[:, kblk * 128:(kblk + 1) * 128],
                                                   B["q"].t[:, q0:q0 + 512], start=True, stop=True),
                      reads=[B["k"], B["q"]], writes=[zb[hf]], inc=(hf == 1))
            e = er.next()
            kb.op(ACT, lambda: nc.scalar.activation(out=e.t[:], in_=zt[:], func=AF.Exp), reads=zb, writes=[e])
            sp = spr.next()
            kb.op(ACT, lambda: nc.scalar.activation(out=sp.t[:], in_=e.t[:], func=AF.Ln, bias=C.oneb.t[:]),
                  reads=[e, C.oneb], writes=[sp])
            if t["diag"] is not None:
                m = C.maskp[t["diag"]]
                kb.op(DVE, lambda: nc.vector.tensor_tensor(out=sp.t[:], in0=sp.t[:], in1=m.t[:], op=ALU.mult),
                      reads=[sp, m], writes=[sp])
            t["sp"] = sp
            if t["j"] == 0:
                t["sacc"] = None
                ns = sar.next()
                kb.op(DVE, lambda: nc.vector.tensor_tensor(out=ns.t[:], in0=sp.t[:, 0:512], in1=sp.t[:, 512:1024],
                                                           op=ALU.add), reads=[sp], writes=[ns])
                state["next_sacc"] = ns
            else:
                t["sacc"] = state["next_sacc"]
                if t["j"] < t["np"] - 1:
                    u = ur.next()
                    kb.op(DVE, lambda: nc.vector.tensor_tensor(out=u.t[:], in0=sp.t[:, 0:512], in1=sp.t[:, 512:1024],
                                                               op=ALU.add), reads=[sp], writes=[u])
                    ns = sar.next()
                    prev = state["next_sacc"]
                    kb.op(DVE, lambda: nc.vector.tensor_tensor(out=ns.t[:], in0=prev.t[:], in1=u.t[:], op=ALU.add),
                          reads=[prev, u], writes=[ns])
                    state["next_sacc"] = ns

        def stageB(idx):
            t = pairs[idx]
            B = hb[t["hd"] % 2]
            q0 = t["qc"] * 512
            lt, lb = C.ps[3], C.psh[3]
            sp, sacc = t["sp"], t["sacc"]
            has_s = sacc is not None
            kA, kB_ = t["kA"], t["kB"]
            kb.op(PE, lambda: nc.tensor.matmul(lt[:, 0:512], B["k"].t[:, kA * 128:(kA + 1) * 128], B["q"].t[:, q0:q0 + 512],
                                               start=True, stop=False), reads=[B["k"], B["q"]], writes=[lb[0]], inc=False)
            kb.op(PE, lambda: nc.tensor.matmul(lt[:, 0:512], C.negtri.t[:], sp.t[:, 0:512], start=False, stop=not has_s),
                  reads=[C.negtri, sp], writes=[lb[0]], inc=False)
            if has_s:
                kb.op(PE, lambda: nc.tensor.matmul(lt[:, 0:512], C.negones.t[:], sacc.t[:], start=False, stop=True),
                      reads=[C.negones, sacc], writes=[lb[0]], inc=False)
            kb.op(PE, lambda: nc.tensor.matmul(lt[:, 512:1024], B["k"].t[:, kB_ * 128:(kB_ + 1) * 128],
                                               B["q"].t[:, q0:q0 + 512], start=True, stop=False),
                  reads=[B["k"], B["q"]], writes=[lb[1]], inc=False)
            kb.op(PE, lambda: nc.tensor.matmul(lt[:, 512:1024], C.negtri.t[:], sp.t[:, 512:1024], start=False, stop=False),
                  reads=[C.negtri, sp], writes=[lb[1]], inc=False)
            kb.op(PE, lambda: nc.tensor.matmul(lt[:, 512:1024], C.negones.t[:], sp.t[:, 0:512], start=False, stop=not has_s),
                  reads=[C.negones, sp], writes=[lb[1]], inc=not has_s)
            if has_s:
                kb.op(PE, lambda: nc.tensor.matmul(lt[:, 512:1024], C.negones.t[:], sacc.t[:], start=False, stop=True),
                      reads=[C.negones, sacc], writes=[lb[1]], inc=True)
            a = ar.next()
            kb.op(ACT, lambda: nc.scalar.activation(out=a.t[:], in_=lt[:], func=AF.Exp), reads=lb, writes=[a])
            if t["diag"] is not None:
                m = C.maskp[t["diag"]]
                kb.op(DVE, lambda: nc.vector.tensor_tensor(out=a.t[:], in0=a.t[:], in1=m.t[:], op=ALU.mult),
                      reads=[a, m], writes=[a])
            t["a"] = a

        def stageC(idx):
            t = pairs[idx]
            B = hb[t["hd"] % 2]
            q0 = t["qc"] * 512
            oi = t["qc"] % 2
            obuf = C.psh[2][oi]
            oap = C.ps[2][:, oi * 512:(oi + 1) * 512]
            last = (t["j"] == t["np"] - 1)
            a = t["a"]
            kb.op(PE, lambda: nc.tensor.matmul(oap, B["v"].t[:, t["kA"], :], a.t[:, 0:512], start=(t["j"] == 0), stop=False),
                  reads=[B["v"], a], writes=[obuf], inc=False)
            kb.op(PE, lambda: nc.tensor.matmul(oap, B["v"].t[:, t["kB"], :], a.t[:, 512:1024], start=False, stop=last),
                  reads=[B["v"], a], writes=[obuf], inc=True)
            if last:
                kb.op(DVE, lambda: nc.vector.tensor_copy(out=B["o"].t[:, q0:q0 + 512], in_=oap),
                      reads=[obuf], writes=[B["o"]])
                if t["qc"] == NQC - 1:
                    kb.dma(SP, d["oh"][t["hd"]], B["o"].t[:], reads=[B["o"]])

        N = len(pairs)
        for step in range(N + 2):
            if step < N:
                stageA(step)
            if 0 <= step - 1 < N:
                stageB(step - 1)
            if 0 <= step - 2 < N:
                stageC(step - 2)


def p3_consts(kb, C, d, gs=None):
    C.gff = (gs or C.gs).sb("gff", [128, 16], F32)
    kb.dma(kb.SP, C.gff.t[:], d["gff"][:, :], writes=[C.gff])


def phase3(kb, C, d, t0):
    nc = kb.nc
    PE, ACT, DVE, SP = kb.PE, kb.ACT, kb.DVE, kb.SP
    xin = d["xT"].rearrange("(c p) t -> p c t", p=128)
    xout = d["xo"].rearrange("(c p) t -> p c t", p=128)
    pr = Ring([0, 1, 2, 3])
    with kb.scope() as S0:
        xg = [S0.sb(f"x3_{i}", [128, 4, TT], F32) for i in range(4)]
        xt = [(xg[c // 4], xg[c // 4].t[:, c % 4, :]) for c in range(16)]
        wring = Ring([S0.sb(f"w3r{i}", [128, 16 * 512], BF16) for i in range(NW)])
        with kb.scope() as S1:
            oT = S1.sb("oT", [128, 8, TT], BF16)
            sT = S1.sb("sT3", [128, 8, TT], BF16)
            kb.dma(SP, oT.t[:], d["oT"].rearrange("(g p) t -> p g t", p=128)[:, :, t0:t0 + TT], writes=[oT])
            kb.dma(SP, sT.t[:], d["sT"].rearrange("(g p) t -> p g t", p=128)[:, :, t0:t0 + TT], writes=[sT])
            for i in range(4):
                kb.dma(SP, xg[i].t[:], xin[:, 4 * i:4 * i + 4, t0:t0 + TT], writes=[xg[i]])
            mg = [S1.sb(f"mg{c}", [128, TT], BF16) for c in range(16)]
            gar = Ring([S1.sb(f"ga{i}", [128, TT], BF16) for i in range(2)])
            gbr = Ring([S1.sb(f"gb{i}", [128, TT], BF16) for i in range(2)])
            m1r = Ring([S1.sb(f"m1{i}", [128, TT], F32) for i in range(2)])
            m2r = Ring([S1.sb(f"m2{i}", [128, TT], F32) for i in range(2)])
            for half in range(2):
                sa = load_wslab(kb, wring, d["w_oa"], 0, 8, half * 1024, 1024)
                sbb = load_wslab(kb, wring, d["w_ob"], 0, 8, half * 1024, 1024)
                for lc in range(8):
                    f = half * 8 + lc
                    pa, pb_ = pr.next(), pr.next()
                    for (pi, slab, act) in ((pa, sa, oT), (pb_, sbb, sT)):
                        for kc in range(8):
                            for h in range(2):
                                kb.op(PE, lambda: nc.tensor.matmul(
                                    C.ps[pi][:, h * 512:(h + 1) * 512],
                                    wslab_view(slab, 8, 1024)[:, kc, lc * 128:(lc + 1) * 128],
                                    act.t[:, kc, h * 512:(h + 1) * 512], start=(kc == 0), stop=(kc == 7)),
                                    reads=[slab, act], writes=[C.psh[pi][h]], inc=(kc == 7 and h == 1))
                    ga, gb = gar.next(), gbr.next()
                    kb.dma(SP, ga.t[:], d["gates"][f * 128:(f + 1) * 128, t0:t0 + TT], writes=[ga])
                    kb.dma(SP, gb.t[:], d["gates"][D + f * 128:D + (f + 1) * 128, t0:t0 + TT], writes=[gb])
                    m1, m2 = m1r.next(), m2r.next()
                    kb.op(DVE, lambda: nc.vector.tensor_tensor(out=m1.t[:], in0=C.ps[pa][:], in1=ga.t[:], op=ALU.mult),
                          reads=[C.psh[pa][0], C.psh[pa][1], ga], writes=[m1])
                    kb.op(DVE, lambda: nc.vector.tensor_tensor(out=m2.t[:], in0=C.ps[pb_][:], in1=gb.t[:], op=ALU.mult),
                          reads=[C.psh[pb_][0], C.psh[pb_][1], gb], writes=[m2])
                    kb.op(DVE, lambda: nc.vector.tensor_tensor(out=mg[f].t[:], in0=m1.t[:], in1=m2.t[:], op=ALU.add),
                          reads=[m1, m2], writes=[mg[f]])
            for s in range(4):
                slab = load_wslab(kb, wring, d["w_out"], 0, 16, s * 512, 512)
                for lc in range(4):
                    f = s * 4 + lc
                    pi = pr.next()
                    for kc in range(16):
                        for h in range(2):
                            kb.op(PE, lambda: nc.tensor.matmul(
                                C.ps[pi][:, h * 512:(h + 1) * 512],
                                wslab_view(slab, 16, 512)[:, kc, lc * 128:(lc + 1) * 128],
                                mg[kc].t[:, h * 512:(h + 1) * 512], start=(kc == 0), stop=(kc == 15)),
                                reads=[slab, mg[kc]], writes=[C.psh[pi][h]], inc=(kc == 15 and h == 1))
                    xb, xap = xt[f]
                    kb.op(DVE, lambda: nc.vector.tensor_tensor(out=xap, in0=xap, in1=C.ps[pi][:], op=ALU.add),
                          reads=[xb, C.psh[pi][0], C.psh[pi][1]], writes=[xb])
        with kb.scope() as S2:
            hT = [S2.sb(f"h2T{c}", [128, TT], BF16) for c in range(16)]
            rms_to_hT(kb, C, S2, xt, C.gff, hT, "f")
            NG = 16
            hid = [[S2.sb(f"hid{i}_{j}", [128, TT], BF16) for j in range(4)] for i in range(2)]
            rr = Ring([S2.sb(f"rr{i}", [128, TT], F32) for i in range(2)])

            def F1(g):
                slab = load_wslab(kb, wring, d["w_ff1"], 0, 16, g * 512, 512)
                for j in range(4):
                    pi = pr.next()
                    for kc in range(16):
                        for h in range(2):
                            kb.op(PE, lambda: nc.tensor.matmul(
                                C.ps[pi][:, h * 512:(h + 1) * 512],
                                wslab_view(slab, 16, 512)[:, kc, j * 128:(j + 1) * 128],
                                hT[kc].t[:, h * 512:(h + 1) * 512], start=(kc == 0), stop=(kc == 15)),
                                reads=[slab, hT[kc]], writes=[C.psh[pi][h]], inc=(kc == 15 and h == 1))
                    r = rr.next()
                    hb_ = hid[g % 2][j]
                    kb.op(ACT, lambda: nc.scalar.activation(out=r.t[:], in_=C.ps[pi][:], func=AF.Relu),
                          reads=[C.psh[pi][0], C.psh[pi][1]], writes=[r])
                    kb.op(DVE, lambda: nc.vector.tensor_tensor(out=hb_.t[:], in0=r.t[:], in1=r.t[:], op=ALU.mult),
                          reads=[r], writes=[hb_])

            def F2(g):
                slab = load_wslab(kb, wring, d["w_ff2"], g * 512, 4, 0, 2048)
                for o in range(16):
                    pi = pr.next()
                    for j in range(4):
                        for h in range(2):
                            kb.op(PE, lambda: nc.tensor.matmul(
                                C.ps[pi][:, h * 512:(h + 1) * 512],
                                wslab_view(slab, 4, 2048)[:, j, o * 128:(o + 1) * 128],
                                hid[g % 2][j].t[:, h * 512:(h + 1) * 512], start=(j == 0), stop=(j == 3)),
                                reads=[slab, hid[g % 2][j]], writes=[C.psh[pi][h]], inc=(j == 3 and h == 1))
                    xb, xap = xt[o]
                    kb.op(DVE, lambda: nc.vector.tensor_tensor(out=xap, in0=xap, in1=C.ps[pi][:], op=ALU.add),
                          reads=[xb, C.psh[pi][0], C.psh[pi][1]], writes=[xb])

            for g in range(NG + 1):
                if g < NG:
                    F1(g)
                if g >= 1:
                    F2(g - 1)
        for i in range(4):
            kb.dma(kb.ACT, xout[:, 4 * i:4 * i + 4, t0:t0 + TT], xg[i].t[:], reads=[xg[i]])


def _in(nc, name, shape, dt):
    return nc.dram_tensor(name, list(shape), dt, kind="ExternalInput").ap()


def _out(nc, name, shape, dt):
    return nc.dram_tensor(name, list(shape), dt, kind="ExternalOutput").ap()


def build_p1(T):
    nc = bass.Bass("TRN2", target_bir_lowering=False)
    d = dict(
        xT=_in(nc, "xT", [D, T], F32), gmix=_in(nc, "gmix", [128, 16], F32), w_in=_in(nc, "w_in", [D, INC], F32),
        gq=_in(nc, "gq", [128, 8], F32), gk=_in(nc, "gk", [128, 8], F32),
        lng=_in(nc, "lng", [1, SGW], F32), lnb=_in(nc, "lnb", [1, SGW], F32),
        wsp=_in(nc, "wsp", [8, 128, 128], F32), bsp=_in(nc, "bsp", [1, 8 * 128], F32),
        qT=_out(nc, "qT", [SBW, T], BF16), kT=_out(nc, "kT", [SBW, T], BF16), v=_out(nc, "v", [T, SBW], BF16),
        sT=_out(nc, "sT", [SGW, T], BF16), gates=_out(nc, "gates", [2 * D, T], BF16),
    )
    with ExitStack() as es:
        kb = KB(nc)
        C = setup_common(kb, es)
        load_consts_eps(kb, C)
        p1_consts(kb, C, d)
        for t0 in range(0, T, TT):
            phase1(kb, C, d, t0)
        kb.finish()
    return nc


def build_p2(S, nheads=2):
    nc = bass.Bass("TRN2", target_bir_lowering=False)
    d = dict(qh=_in(nc, "qh", [nheads, 128, S], BF16), kh=_in(nc, "kh", [nheads, 128, S], BF16),
             vh=_in(nc, "vh", [nheads, S, 128], BF16), oh=_out(nc, "oh", [nheads, 128, S], BF16))
    with ExitStack() as es:
        kb = KB(nc)
        C = setup_common(kb, es)
        load_consts_eps(kb, C)
        p2_consts(kb, C)
        phase2(kb, C, d, S, nheads)
        kb.finish()
    return nc


def build_p3(T):
    nc = bass.Bass("TRN2", target_bir_lowering=False)
    d = dict(
        xT=_in(nc, "xT", [D, T], F32), oT=_in(nc, "oT", [SBW, T], BF16), sT=_in(nc, "sT", [SGW, T], BF16),
        gates=_in(nc, "gates", [2 * D, T], BF16), w_oa=_in(nc, "w_oa", [SBW, D], F32),
        w_ob=_in(nc, "w_ob", [SGW, D], F32), w_out=_in(nc, "w_out", [D, D], F32), gff=_in(nc, "gff", [128, 16], F32),
        w_ff1=_in(nc, "w_ff1", [D, DFF], F32), w_ff2=_in(nc, "w_ff2", [DFF, D], F32),
        xo=_out(nc, "xo", [D, T], F32),
    )
    with ExitStack() as es:
        kb = KB(nc)
        C = setup_common(kb, es)
        load_consts_eps(kb, C)
        p3_consts(kb, C, d)
        for t0 in range(0, T, TT):
            phase3(kb, C, d, t0)
        kb.finish()
    return nc


def build_fused(S, depth):
    nc = bass.Bass("TRN2", target_bir_lowering=False)
    x_in = _in(nc, "xT", [D, S], F32)
    W = dict(
        gmix=_in(nc, "gmix", [depth, 128, 16], F32), w_in=_in(nc, "w_in", [depth, D, INC], F32),
        gq=_in(nc, "gq", [depth, 128, 8], F32), gk=_in(nc, "gk", [depth, 128, 8], F32),
        lng=_in(nc, "lng", [depth, 1, SGW], F32), lnb=_in(nc, "lnb", [depth, 1, SGW], F32),
        wsp=_in(nc, "wsp", [depth, 8, 128, 128], F32), bsp=_in(nc, "bsp", [depth, 1, 8 * 128], F32),
        w_oa=_in(nc, "w_oa", [depth, SBW, D], F32), w_ob=_in(nc, "w_ob", [depth, SGW, D], F32),
        w_out=_in(nc, "w_out", [depth, D, D], F32), gff=_in(nc, "gff", [depth, 128, 16], F32),
        w_ff1=_in(nc, "w_ff1", [depth, D, DFF], F32), w_ff2=_in(nc, "w_ff2", [depth, DFF, D], F32))
    y_out = _out(nc, "yT", [D, S], F32)
    xbuf = nc.dram_tensor("xbuf", [D, S], F32).ap()
    qT = nc.dram_tensor("qT_s", [SBW, S], BF16).ap()
    kT = nc.dram_tensor("kT_s", [SBW, S], BF16).ap()
    v = nc.dram_tensor("v_s", [S, SBW], BF16).ap()
    sT = nc.dram_tensor("sT_s", [SGW, S], BF16).ap()
    gates = nc.dram_tensor("gates_s", [2 * D, S], BF16).ap()
    oT = nc.dram_tensor("oT_s", [SBW, S], BF16).ap()
    with ExitStack() as es:
        kb = KB(nc)
        C = setup_common(kb, es)
        load_consts_eps(kb, C)
        p2_consts(kb, C)
        for l in range(depth):
            src = x_in if l == 0 else xbuf
            dst = y_out if l == depth - 1 else xbuf
            d = {k: a[l] for k, a in W.items()}
            d.update(xT=src, xo=dst, qT=qT, kT=kT, v=v, sT=sT, gates=gates, oT=oT)
            d2 = dict(qh=qT.rearrange("(h p) s -> h p s", p=128), kh=kT.rearrange("(h p) s -> h p s", p=128),
                      vh=v.rearrange("s (h x) -> h s x", h=NH), oh=oT.rearrange("(h p) s -> h p s", p=128))
            with kb.scope() as LS:
                p1_consts(kb, C, d, LS)
                for t0 in range(0, S, TT):
                    phase1(kb, C, d, t0)
            kb.barrier()
            phase2(kb, C, d2, S, NH)
            kb.barrier()
            with kb.scope() as LS:
                p3_consts(kb, C, d, LS)
                for t0 in range(0, S, TT):
                    phase3(kb, C, d, t0)
            kb.barrier()
        kb.finish()
    return nc


def kernel_fused(x, g_mix, w_in, g_q, g_k, sgu_ln_g, sgu_ln_b, w_spatial, b_spatial, w_oa, w_ob, w_out, g_ff,
                 w_ff1, w_ff2):
    x = np.asarray(x)
    B, S, _ = x.shape
    f32 = np.float32
    depth = np.asarray(w_in).shape[0]
    gpb = NCORES // B
    cores = list(range(NCORES))
    shared = dict(
        gmix=_c(np.asarray(g_mix, f32).reshape(depth, 16, 128).transpose(0, 2, 1)), w_in=_c(np.asarray(w_in, f32)),
        gq=_c(np.asarray(g_q, f32).transpose(0, 2, 1)), gk=_c(np.asarray(g_k, f32).transpose(0, 2, 1)),
        lng=_c(np.asarray(sgu_ln_g, f32).reshape(depth, 1, SGW)), lnb=_c(np.asarray(sgu_ln_b, f32).reshape(depth, 1, SGW)),
        wsp=_c(np.asarray(w_spatial, f32)), bsp=_c(np.asarray(b_spatial, f32).reshape(depth, 1, 8 * 128)),
        w_oa=_c(np.asarray(w_oa, f32)), w_ob=_c(np.asarray(w_ob, f32)), w_out=_c(np.asarray(w_out, f32)),
        gff=_c(np.asarray(g_ff, f32).reshape(depth, 16, 128).transpose(0, 2, 1)),
        w_ff1=_c(np.asarray(w_ff1, f32)), w_ff2=_c(np.asarray(w_ff2, f32)))
    xTs = [_c(x[b].T) for b in range(B)]
    nc = _prog(("fused", S, depth), lambda: build_fused(S, depth))
    res = run_bass_kernel_spmd(nc, [dict(shared, xT=xTs[c // gpb]) for c in cores], core_ids=cores).results
    out = np.empty((B, S, D), dtype=np.float32)
    for b in range(B):
        out[b] = res[b * gpb]["yT"].T
    return out


_PROGS = {}


def _prog(key, fn):
    if key not in _PROGS:
        _PROGS[key] = fn()
    return _PROGS[key]


def _c(a):
    return np.ascontiguousarray(a)


def kernel_unfused(x, g_mix, w_in, g_q, g_k, sgu_ln_g, sgu_ln_b, w_spatial, b_spatial, w_oa, w_ob, w_out, g_ff, w_ff1,
           w_ff2):
    x = np.asarray(x)
    B, S, _ = x.shape
    depth = np.asarray(w_in).shape[0]
    gpb = NCORES // B
    T = S // gpb
    hpc = NH // gpb
    cores = list(range(NCORES))
    xT = [_c(x[c // gpb, (c % gpb) * T:(c % gpb + 1) * T, :].T) for c in cores]
    f32 = np.float32
    for l in range(depth):
        wl = _c(np.asarray(w_in[l], dtype=f32))
        common1 = dict(
            gmix=_c(np.asarray(g_mix[l], f32).reshape(16, 128).T), w_in=wl,
            gq=_c(np.asarray(g_q[l], f32).T), gk=_c(np.asarray(g_k[l], f32).T),
            lng=_c(np.asarray(sgu_ln_g[l], f32).reshape(1, SGW)), lnb=_c(np.asarray(sgu_ln_b[l], f32).reshape(1, SGW)),
            wsp=_c(np.asarray(w_spatial[l], f32)), bsp=_c(np.asarray(b_spatial[l], f32).reshape(1, 8 * 128)))
        r1 = run_bass_kernel_spmd(_prog(("p1", T), lambda: build_p1(T)),
                                  [dict(common1, xT=xT[c]) for c in cores], core_ids=cores).results
        in2 = []
        for c in cores:
            b, hp = c // gpb, c % gpb
            src = [r1[b * gpb + j] for j in range(gpb)]
            rows = slice(hp * hpc * 128, (hp + 1) * hpc * 128)
            qh = np.concatenate([s_["qT"][rows] for s_ in src], axis=1).reshape(hpc, 128, S)
            kh = np.concatenate([s_["kT"][rows] for s_ in src], axis=1).reshape(hpc, 128, S)
            vh = np.concatenate([s_["v"][:, rows] for s_ in src], axis=0).reshape(S, hpc, 128).transpose(1, 0, 2)
            in2.append(dict(qh=_c(qh), kh=_c(kh), vh=_c(vh)))
        r2 = run_bass_kernel_spmd(_prog(("p2", S, hpc), lambda: build_p2(S, hpc)), in2, core_ids=cores).results
        common3 = dict(w_oa=_c(np.asarray(w_oa[l], f32)), w_ob=_c(np.asarray(w_ob[l], f32)),
                       w_out=_c(np.asarray(w_out[l], f32)), gff=_c(np.asarray(g_ff[l], f32).reshape(16, 128).T),
                       w_ff1=_c(np.asarray(w_ff1[l], f32)), w_ff2=_c(np.asarray(w_ff2[l], f32)))
        in3 = []
        for c in cores:
            b, j = c // gpb, c % gpb
            oT = np.concatenate([r2[b * gpb + hp]["oh"][:, :, j * T:(j + 1) * T].reshape(hpc * 128, T)
                                 for hp in range(gpb)], axis=0)
            in3.append(dict(common3, xT=xT[c], oT=_c(oT), sT=r1[c]["sT"], gates=r1[c]["gates"]))
        r3 = run_bass_kernel_spmd(_prog(("p3", T), lambda: build_p3(T)), in3, core_ids=cores).results
        xT = [r3[c]["xo"] for c in cores]
    out = np.empty((B, S, D), dtype=np.float32)
    for c in cores:
        out[c // gpb, (c % gpb) * T:(c % gpb + 1) * T, :] = xT[c].T
    return out


FUSED = False


def kernel(**inputs):
    return kernel_fused(**inputs) if FUSED else kernel_unfused(**inputs)
```
